# Optimizing a Trainium2 kernel written in Bass

```python
import jax, jax.numpy as jnp
from jax import lax
import numpy as np

D_MODEL = 1024
BATCH = 1
SEQ = 16384
DEPTH = 4
DEC_BATCH = 2
DEC_SEQ = 8192
PAST_LEN = 128

GRID_W = 64
NA_HEADS = 16
HEAD_DIM = D_MODEL // NA_HEADS
MAX_KH = 8
KW = 16
N_MIXERS = 2
CHUNK = 128
SG_GROUPS = 16
SG_WIDTH = D_MODEL
SG_GROUP_DIM = SG_WIDTH // SG_GROUPS
D_FF = ((8 * D_MODEL + 3 * 256 - 1) // (3 * 256)) * 256
N_NA_LAYERS = (DEPTH + N_MIXERS - 1) // N_MIXERS
N_SG_LAYERS = DEPTH // N_MIXERS
ALPHA = (2 * DEPTH) ** 0.25
BETA = (8 * DEPTH) ** -0.25
LN_EPS = 1e-5

kernel_name = "hybrid_natten_gmlp_deepnorm_encoder"


def layer_norm(x, g, b):
    xf = x.astype(jnp.float32)
    mu = jnp.mean(xf, axis=-1, keepdims=True)
    var = jnp.mean(jnp.square(xf - mu), axis=-1, keepdims=True)
    y = (xf - mu) * lax.rsqrt(var + LN_EPS)
    return (y * g.astype(jnp.float32) + b.astype(jnp.float32)).astype(x.dtype)


def neighbourhood_attention(x, w_in, rpb, w_out):
    B, T, D = x.shape
    rows = T // GRID_W
    kh = min(MAX_KH, rows)
    q, k, v = jnp.split(x @ w_in, 3, axis=-1)
    q = q.reshape(B, rows, GRID_W, NA_HEADS, HEAD_DIM) * (HEAD_DIM ** -0.5)
    k = k.reshape(B, rows, GRID_W, NA_HEADS, HEAD_DIM)
    v = v.reshape(B, rows, GRID_W, NA_HEADS, HEAD_DIM)
    r_ar = np.arange(rows)
    row_start = np.clip(r_ar - kh // 2, 0, rows - kh)
    dr_idx = row_start[:, None] + np.arange(kh)[None, :] - r_ar[:, None] + MAX_KH - 1
    c_ar = np.arange(GRID_W)
    col_start = np.clip(c_ar - KW // 2, 0, GRID_W - KW)
    col_idx = col_start[:, None] + np.arange(KW)[None, :]
    dc_idx = col_idx - c_ar[:, None] + KW - 1
    rpb_c = rpb[:, :, dc_idx]

    def row_fn(args):
        q_r, rs, dr = args
        k_rows = lax.dynamic_slice_in_dim(k, rs, kh, axis=1)
        v_rows = lax.dynamic_slice_in_dim(v, rs, kh, axis=1)
        k_win = k_rows[:, :, col_idx]
        v_win = v_rows[:, :, col_idx]
        bias = jnp.take(rpb_c, dr, axis=1)
        s = jnp.einsum('bchd,bjckhd->bhjck', q_r, k_win).astype(jnp.float32) + bias.astype(jnp.float32)
        p = jax.nn.softmax(s, axis=(2, 4)).astype(v.dtype)
        return jnp.einsum('bhjck,bjckhd->bchd', p, v_win)

    o = lax.map(row_fn, (jnp.moveaxis(q, 1, 0),
                         jnp.asarray(row_start, dtype=jnp.int32),
                         jnp.asarray(dr_idx, dtype=jnp.int32)))
    o = jnp.moveaxis(o, 0, 1).reshape(B, T, D)
    return o @ w_out


def spatial_gating(x, w_in, ln_g, ln_b, w_s, b_s, w_out):
    B, T, D = x.shape
    n_chunks = T // CHUNK
    u, v = jnp.split(jax.nn.gelu(x @ w_in, approximate=False), 2, axis=-1)
    v = layer_norm(v, ln_g, ln_b).reshape(B, n_chunks, CHUNK, SG_GROUPS, SG_GROUP_DIM)
    v = jnp.einsum('gpq,bnqgc->bnpgc', w_s, v) + b_s.T[None, None, :, :, None]
    return (u * v.reshape(B, T, SG_WIDTH)) @ w_out


def swiglu(x, w_in, w_out):
    g, h = jnp.split(x @ w_in, 2, axis=-1)
    return (jax.nn.silu(g) * h) @ w_out


def trunk(x, na_w_in, na_rpb, na_w_out, sg_w_in, sg_ln_g, sg_ln_b, sg_w_s, sg_b_s, sg_w_out,
          ln_mix_g, ln_mix_b, ffn_w_in, ffn_w_out, ln_ffn_g, ln_ffn_b):
    for i in range(DEPTH):
        j = i // N_MIXERS
        if i % N_MIXERS == 0:
            m = neighbourhood_attention(x, na_w_in[j], na_rpb[j], na_w_out[j])
        else:
            m = spatial_gating(x, sg_w_in[j], sg_ln_g[j], sg_ln_b[j], sg_w_s[j], sg_b_s[j], sg_w_out[j])
        x = layer_norm(ALPHA * x + m, ln_mix_g[i], ln_mix_b[i])
        x = layer_norm(ALPHA * x + swiglu(x, ffn_w_in[i], ffn_w_out[i]), ln_ffn_g[i], ln_ffn_b[i])
    return x


def setup_inputs(seed: int = 0) -> dict:
    key = jax.random.key(seed)
    ks = jax.random.split(key, 24)
    f32 = jnp.float32
    nrm = lambda k, shape, s: jax.random.normal(k, shape, f32) * s
    D = D_MODEL
    x_prompt = nrm(ks[0], (BATCH, SEQ, D), 1.0)
    x_sample = nrm(ks[1], (DEC_BATCH, DEC_SEQ, D), 1.0)
    na_w_in = jnp.concatenate([nrm(ks[2], (N_NA_LAYERS, D, 2 * D), D ** -0.5),
                               nrm(ks[3], (N_NA_LAYERS, D, D), BETA * D ** -0.5)], axis=-1)
    na_rpb = nrm(ks[4], (N_NA_LAYERS, NA_HEADS, 2 * MAX_KH - 1, 2 * KW - 1), 0.1)
    na_w_out = nrm(ks[5], (N_NA_LAYERS, D, D), BETA * D ** -0.5)
    sg_w_in = nrm(ks[6], (N_SG_LAYERS, D, 2 * SG_WIDTH), D ** -0.5)
    sg_ln_g = 1.0 + nrm(ks[7], (N_SG_LAYERS, SG_WIDTH), 0.01)
    sg_ln_b = nrm(ks[8], (N_SG_LAYERS, SG_WIDTH), 0.01)
    sg_w_s = nrm(ks[9], (N_SG_LAYERS, SG_GROUPS, CHUNK, CHUNK), CHUNK ** -0.5)
    sg_b_s = 1.0 + nrm(ks[10], (N_SG_LAYERS, SG_GROUPS, CHUNK), 0.1)
    sg_w_out = nrm(ks[11], (N_SG_LAYERS, SG_WIDTH, D), BETA * SG_WIDTH ** -0.5)
    ln_mix_g = 1.0 + nrm(ks[12], (DEPTH, D), 0.01)
    ln_mix_b = nrm(ks[13], (DEPTH, D), 0.01)
    ffn_w_in = nrm(ks[14], (DEPTH, D, 2 * D_FF), D ** -0.5)
    ffn_w_out = nrm(ks[15], (DEPTH, D_FF, D), BETA * D_FF ** -0.5)
    ln_ffn_g = 1.0 + nrm(ks[16], (DEPTH, D), 0.01)
    ln_ffn_b = nrm(ks[17], (DEPTH, D), 0.01)
    return {"x_prompt": x_prompt, "x_sample": x_sample,
            "na_w_in": na_w_in, "na_rpb": na_rpb, "na_w_out": na_w_out,
            "sg_w_in": sg_w_in, "sg_ln_g": sg_ln_g, "sg_ln_b": sg_ln_b,
            "sg_w_s": sg_w_s, "sg_b_s": sg_b_s, "sg_w_out": sg_w_out,
            "ln_mix_g": ln_mix_g, "ln_mix_b": ln_mix_b,
            "ffn_w_in": ffn_w_in, "ffn_w_out": ffn_w_out,
            "ln_ffn_g": ln_ffn_g, "ln_ffn_b": ln_ffn_b}


def reference(x_prompt, x_sample, na_w_in, na_rpb, na_w_out, sg_w_in, sg_ln_g, sg_ln_b,
              sg_w_s, sg_b_s, sg_w_out, ln_mix_g, ln_mix_b, ffn_w_in, ffn_w_out,
              ln_ffn_g, ln_ffn_b):
    y_prompt = trunk(x_prompt, na_w_in, na_rpb, na_w_out, sg_w_in, sg_ln_g, sg_ln_b, sg_w_s,
                     sg_b_s, sg_w_out, ln_mix_g, ln_mix_b, ffn_w_in, ffn_w_out, ln_ffn_g, ln_ffn_b)
    y_sample = trunk(x_sample, na_w_in, na_rpb, na_w_out, sg_w_in, sg_ln_g, sg_ln_b, sg_w_s,
                     sg_b_s, sg_w_out, ln_mix_g, ln_mix_b, ffn_w_in, ffn_w_out, ln_ffn_g, ln_ffn_b)
    return (y_prompt, y_sample)
```

```python
import numpy as np
from contextlib import ExitStack
import concourse.bass as bass
import concourse.mybir as mybir
from concourse.bass_utils import run_bass_kernel_spmd

F32 = mybir.dt.float32
BF16 = mybir.dt.bfloat16
AF = mybir.ActivationFunctionType
ALU = mybir.AluOpType

D = 1024
NTOK = 5120
ALPHA = float(8.0 ** 0.25)
EPS = 1e-5
NEG = -30000.0
ENGS = ["pe", "act", "dve", "pool", "sp"]
ARENA_BYTES = 189 * 1024


class Op:
    __slots__ = ("eng", "fn", "dma", "slot", "deps", "needs_inc", "sem", "val", "grp")

    def __init__(self, eng, fn, dma, slot):
        self.eng = eng
        self.fn = fn
        self.dma = dma
        self.slot = slot
        self.grp = None
        self.deps = []
        self.needs_inc = False
        self.sem = None
        self.val = None


class Prog:
    def __init__(self, nc):
        self.nc = nc
        self.ops = []
        self.last_writer = {}
        self.readers = {}
        self.final_dmas = []
        self.last_eng = {}
        self.last_slot = {}
        self.bar = None
        self.bar_done = set()

    def barrier(self):
        deps = list(self.last_eng.values()) + list(self.last_slot.values())
        self.bar = deps
        self.bar_done = set()
        self.last_writer = {}
        self.readers = {}

    def op(self, eng, fn, reads=(), writes=(), dma=False, slot=None, final=False):
        o = Op(eng, fn, dma, slot)
        deps = set()
        for r in reads:
            w = self.last_writer.get(r)
            if w is not None:
                deps.add(w)
        for r in writes:
            w = self.last_writer.get(r)
            if w is not None:
                deps.add(w)
            for rd in self.readers.get(r, ()):
                deps.add(rd)
        if self.bar is not None and eng not in self.bar_done:
            self.bar_done.add(eng)
            for d in self.bar:
                deps.add(d)
        for d in deps:
            if (not o.dma) and (not d.dma) and o.eng == "pe" and d.eng == "pe":
                continue
            o.deps.append(d)
        for r in writes:
            self.last_writer[r] = o
            self.readers[r] = []
        for r in reads:
            self.readers.setdefault(r, []).append(o)
        self.ops.append(o)
        if dma:
            self.last_slot[slot] = o
        else:
            self.last_eng[eng] = o
        if final:
            self.final_dmas.append(o)
        return o

    def dma(self, q, out, in_, reads=(), writes=(), slot=None, final=False, grp=None):
        if grp is not None:
            slot = grp
        elif slot is None:
            slot = writes[0] if writes else reads[0]
        slot = q + ":" + slot
        o = self.op(q, lambda e, out=out, in_=in_: e.dma_start(out=out, in_=in_),
                    reads=reads, writes=writes, dma=True, slot=slot, final=final)
        o.grp = grp
        return o

    def mm(self, out, lhsT, rhs, start, stop, reads, writes, tp=None):
        if tp is None:
            fn = lambda e: e.matmul(out, lhsT, rhs, start=start, stop=stop)
        else:
            fn = lambda e: e.matmul(out, lhsT, rhs, start=start, stop=stop, tile_position=tp)
        return self.op("pe", fn, reads=reads, writes=writes)

    def emit(self):
        nc = self.nc
        for o in self.ops:
            for d in o.deps:
                d.needs_inc = True
        for o in self.final_dmas:
            o.needs_inc = True
        slots = []
        seen = set()
        for o in self.ops:
            if o.dma and o.needs_inc and o.slot not in seen:
                seen.add(o.slot)
                slots.append(o.slot)
        self.n_slots = len(slots)
        with ExitStack() as es:
            esem = {e: es.enter_context(nc.semaphore("e_" + e)) for e in ENGS}
            ssem = {s: es.enter_context(nc.semaphore("d%d" % i)) for i, s in enumerate(slots)}
            ecnt = {e: 0 for e in ENGS}
            scnt = {s: 0 for s in slots}
            for o in self.ops:
                if o.dma and o.grp is not None and o.slot in ssem:
                    o.needs_inc = True
                if not o.needs_inc:
                    continue
                if o.dma:
                    scnt[o.slot] += 16
                    o.sem = ssem[o.slot]
                    o.val = scnt[o.slot]
                else:
                    ecnt[o.eng] += 1
                    o.sem = esem[o.eng]
                    o.val = ecnt[o.eng]
            run = []
            for o in self.ops + [None]:
                if o is not None and o.dma and o.grp is not None and o.needs_inc:
                    if run and run[-1].slot != o.slot:
                        for r in run:
                            r.val = run[-1].val
                        run = []
                    run.append(o)
                elif o is None or (o.dma and o.needs_inc):
                    for r in run:
                        r.val = run[-1].val
                    run = []
            block = es.enter_context(nc.Block())
            per = {e: [o for o in self.ops if o.eng == e] for e in ENGS}
            engobj = {"pe": nc.tensor, "act": nc.scalar, "dve": nc.vector, "pool": nc.gpsimd,
                      "sp": nc.sync}
            finals = self.final_dmas

            def make(e):
                def body(h):
                    waited = {}
                    for o in per[e]:
                        need = {}
                        for d in o.deps:
                            k = id(d.sem)
                            if k not in need or need[k][1] < d.val:
                                need[k] = (d.sem, d.val)
                        for k, (sem, val) in need.items():
                            if waited.get(k, 0) >= val:
                                continue
                            h.wait_ge(sem, val)
                            waited[k] = val
                        ins = o.fn(engobj[e])
                        if o.needs_inc:
                            ins.then_inc(o.sem, 16 if o.dma else 1)
                    if e == "sp":
                        for o in finals:
                            h.wait_ge(o.sem, o.val)
                return body

            block.tensor(make("pe"))
            block.scalar(make("act"))
            block.vector(make("dve"))
            block.gpsimd(make("pool"))
            block.sync(make("sp"))


def band_ranges(kind):
    r = [(max(0, 2 * p - 7), min(7, 2 * p + 1)) for p in range(8)]
    ext = {
        None: {},
        "s0": {6: (4, 7), 7: (4, 7)},
        "e0": {0: (0, 3)},
        "s2": {4: (0, 7), 5: (0, 7)},
        "e2": {2: (0, 7)},
    }[kind]
    for p, v in ext.items():
        r[p] = (min(r[p][0], v[0]), max(r[p][1], v[1]))
    return r


class Arena:
    def __init__(self, t):
        self.t = t
        self.off = 0

    def reset(self):
        self.off = 0

    def alloc(self, shape, dt):
        esz = 4 if dt == F32 else 2
        n = 1
        for s in shape[1:]:
            n *= s
        nbytes = (n * esz + 63) // 64 * 64
        assert self.off + nbytes <= ARENA_BYTES, ("arena overflow", self.off + nbytes)
        v = self.t[:, self.off // 2:(self.off + n * esz) // 2]
        self.off += nbytes
        if dt == F32:
            v = v.bitcast(F32)
        if len(shape) == 3:
            v = v.rearrange("p (a b) -> p a b", a=shape[1])
        elif len(shape) == 4:
            v = v.rearrange("p (a b c) -> p a b c", a=shape[1], b=shape[2])
        return v


def build_program():
    nc = bass.Bass("TRN2", target_bir_lowering=False)

    def din(name, shape):
        return nc.dram_tensor(name, shape, F32, kind="ExternalInput").ap()

    def dscr(name, shape, dt):
        return nc.dram_tensor(name, shape, dt, kind="Internal").ap()

    x0T = din("x0T", [D, NTOK])
    wqkv = din("wqkv", [2, 128, 8 * 3072])
    wona = din("wona", [2, 128, 8 * 1024])
    wsgi = din("wsgi", [2, 128, 8 * 2048])
    wsgo = din("wsgo", [2, 128, 8 * 1024])
    wsT = din("wsT", [2, 128, 16 * 128])
    bsb = din("bsb", [2, 128, 8 * 128])
    sgfm_d = din("sgfm", [128, 32])
    wfi = din("wfi", [4, 11, 128, 4 * 8 * 128])
    wfo = din("wfo", [4, 8, 128, 22 * 128])
    lnp_d = din("lnp", [128, 4 * 4 * 8])
    trt = din("trt", [2, 16, 128, 22 * 64])
    msk = din("msk", [2, 9, 8, 1024])
    qdl = din("qdl", [8, 512])
    yT = nc.dram_tensor("yT", [D, 4096], F32, kind="ExternalOutput").ap()

    XS = [dscr("xs0", [D, NTOK], F32), dscr("xs1", [D, NTOK], F32)]
    XSb = [dscr("xsb0", [D, NTOK], BF16), dscr("xsb1", [D, NTOK], BF16)]
    QT = dscr("qt", [D, NTOK], BF16)
    KT = dscr("kt", [D, NTOK], BF16)
    VV = dscr("vv", [NTOK, D], BF16)
    ETS = dscr("ets", [2, 16, 128, 22 * 64], F32)

    def fm(ap):
        return ap.rearrange("(f p) t -> p f t", p=128)

    with ExitStack() as es:
        arena_t = es.enter_context(nc.sbuf_tensor("arena", [128, ARENA_BYTES // 2], BF16))
        ONESM = es.enter_context(nc.sbuf_tensor("onesm", [128, 128], BF16))
        ONES1 = es.enter_context(nc.sbuf_tensor("ones1", [128, 64], BF16))
        LNP = es.enter_context(nc.sbuf_tensor("lnp_sb", [128, 128], F32))
        SGFM = es.enter_context(nc.sbuf_tensor("sgfm_sb", [128, 32], F32))
        ONES128 = es.enter_context(nc.sbuf_tensor("ones128", [128, 128], BF16))
        ps = [es.enter_context(nc.psum_tensor("ps%d" % i, [128, 512], F32)) for i in range(8)]
        PSK = ["ps%d" % i for i in range(8)]
        A = Arena(arena_t)
        P = Prog(nc)

        P.op("pool", lambda e: e.memset(ONESM[:], 1.0 / 1024.0), writes=["onesm"])
        P.op("pool", lambda e: e.memset(ONES1[:], 1.0), writes=["ones1"])
        P.dma("sp", LNP[:], lnp_d, writes=["lnp"])
        P.dma("sp", SGFM[:], sgfm_d, writes=["sgfm"])
        P.op("pool", lambda e: e.memset(ONES128[:], 1.0), writes=["ones128"])

        def lnp_col(tab, l, f):
            c = (tab * 4 + l) * 8 + f
            return LNP[:, c:c + 1]

        class LNState:
            pass

        def ln_alloc(n_nt):
            S = LNState()
            S.sbt = [A.alloc([128, 512], BF16) for _ in range(4)]
            S.s2t = [A.alloc([128, 512], BF16) for _ in range(4)]
            S.ms = [A.alloc([128, 512], F32) for _ in range(n_nt)]
            S.m2 = A.alloc([128, 512], F32)
            S.var = [A.alloc([128, 512], F32) for _ in range(n_nt)]
            S.a = [A.alloc([128, 512], F32) for _ in range(n_nt)]
            S.b = [A.alloc([128, 512], F32) for _ in range(n_nt)]
            S.t1 = [A.alloc([128, 512], F32) for _ in range(2)]
            S.cnt = 0
            S.tcnt = 0
            S.pending = []
            S.tail = []
            return S

        def drain(S, n=None):
            while S.tail and (n is None or n > 0):
                S.tail.pop(0)[1]()
                if n is not None:
                    n -= 1

        BANK_KEYS = {3: ["O0_0", "O0_1"], 4: ["Z0_0", "Z0_1"], 5: ["O1_0", "O1_1"], 6: ["Z1_0", "Z1_1"]}

        def bk(S, b):
            return [PSK[b]] + (BANK_KEYS.get(b, []) if getattr(S, "oz_banks", False) else [])

        def epi_stats(S, ent):
            f, i, mean_b, ex2_b = ent
            P.mm(ps[mean_b][:, :], ONESM[:, :], S.sbt[i], f == 0, f == 7,
                 reads=["sbt%d" % i, "onesm"], writes=bk(S, mean_b))
            P.mm(ps[ex2_b][:, :], ONESM[:, :], S.s2t[i], f == 0, f == 7,
                 reads=["s2t%d" % i, "onesm"], writes=bk(S, ex2_b))

        def epi_a(S, f, y_ps, y_key, xf_f, xf_key, mean_b, ex2_b, use_pool=True, lag=2):
            i = S.cnt % 4
            S.cnt += 1
            P.op("dve", lambda e: e.scalar_tensor_tensor(out=xf_f, in0=xf_f, scalar=ALPHA, in1=y_ps,
                                                         op0=ALU.mult, op1=ALU.add),
                 reads=(y_key if isinstance(y_key, list) else [y_key]) + [xf_key], writes=[xf_key])
            sbt, s2t = S.sbt[i], S.s2t[i]
            P.op("act", lambda e: e.copy(sbt, xf_f), reads=[xf_key], writes=["sbt%d" % i])
            if use_pool:
                P.op("pool", lambda e: e.tensor_tensor(out=s2t, in0=xf_f, in1=xf_f, op=ALU.mult),
                     reads=[xf_key], writes=["s2t%d" % i])
            else:
                P.op("act", lambda e: e.activation(out=s2t, in_=xf_f, func=AF.Square),
                     reads=[xf_key], writes=["s2t%d" % i])
            S.pending.append((f, i, mean_b, ex2_b))
            while len(S.pending) > lag:
                epi_stats(S, S.pending.pop(0))

        def epi_flush(S):
            while S.pending:
                epi_stats(S, S.pending.pop(0))

        def epi_b(S, xf, xf_keys, mean_b, ex2_b, gtab, btab, l, dst_f32, dst_bf16, final=False,
                  use_pool=True, nt=0, after=None, defer_head=False):
            epi_flush(S)
            ms, sa, sb_, var = S.ms[nt], S.a[nt], S.b[nt], S.var[nt]
            kms, ka_, kb_, kv = "ms%d" % nt, "lna%d" % nt, "lnb%d" % nt, "var%d" % nt
            P.op("act", lambda e: e.copy(ms, ps[mean_b][:, :]), reads=bk(S, mean_b), writes=[kms])
            P.op("pool" if use_pool else "dve",
                 lambda e: e.tensor_tensor(out=S.m2, in0=ms, in1=ms, op=ALU.mult),
                 reads=[kms], writes=["m2"])
            P.op("dve", lambda e: e.tensor_tensor(out=var, in0=ps[ex2_b][:, :], in1=S.m2,
                                                  op=ALU.subtract),
                 reads=bk(S, ex2_b) + ["m2"], writes=[kv])

            def head2():
                P.op("dve", lambda e: e.tensor_scalar_add(out=var, in0=var, scalar1=EPS),
                     reads=[kv], writes=[kv])
                P.op("act", lambda e: e.activation(out=var, in_=var, func=AF.Ln),
                     reads=[kv], writes=[kv])
                P.op("act", lambda e: e.activation(out=sa, in_=var, func=AF.Exp, scale=-0.5),
                     reads=[kv], writes=[ka_])
                P.op("dve", lambda e: e.scalar_tensor_tensor(out=sb_, in0=ms, scalar=-1.0, in1=sa,
                                                             op0=ALU.mult, op1=ALU.mult),
                     reads=[kms, ka_], writes=[kb_])

            if defer_head:
                k = 0
                while k < len(S.tail) and S.tail[k][0] == "h":
                    k += 1
                S.tail.insert(k, ("h", head2))
            else:
                head2()

            tis = {}

            def dve_part(f):
                ti = S.tcnt % 2
                S.tcnt += 1
                tis[f] = ti
                t1 = S.t1[ti]
                tk = "t1_%d" % ti
                xf_f = xf[:, f, :]
                eng = "dve" if (f % 2 == 0 or not use_pool) else "pool"
                P.op(eng, lambda e: e.tensor_tensor(out=t1, in0=xf_f, in1=sa, op=ALU.mult),
                     reads=[xf_keys[f], ka_], writes=[tk])
                P.op(eng, lambda e: e.tensor_tensor(out=t1, in0=t1, in1=sb_, op=ALU.add),
                     reads=[tk, kb_], writes=[tk])

            def act_part(f):
                ti = tis[f]
                t1 = S.t1[ti]
                tk = "t1_%d" % ti
                xf_f = xf[:, f, :]
                g = lnp_col(gtab, l, f)
                b = lnp_col(btab, l, f)
                P.op("act", lambda e: e.activation(out=xf_f, in_=t1, func=AF.Identity, bias=b, scale=g),
                     reads=[tk, "lnp"], writes=[xf_keys[f]])

            def apply_f(f):
                dve_part(f)
                if f >= 1:
                    act_part(f - 1)

            def stores():
                act_part(7)
                for hf in range(2):
                    P.dma("sp", dst_f32[:, 4 * hf:4 * hf + 4, :], xf[:, 4 * hf:4 * hf + 4, :],
                          reads=xf_keys[4 * hf:4 * hf + 4], writes=[], slot=xf_keys[4 * hf] + "_st",
                          final=final)
                if dst_bf16 is not None:
                    for hf in range(2):
                        P.dma("pool", dst_bf16[:, 4 * hf:4 * hf + 4, :], xf[:, 4 * hf:4 * hf + 4, :],
                              reads=xf_keys[4 * hf:4 * hf + 4], writes=[], slot=xf_keys[4 * hf] + "_sb")
                if after is not None:
                    after()

            for f in range(8):
                S.tail.append(("a", lambda f=f: apply_f(f)))
            S.tail.append(("s", stores))

        def qkv_pass(li, src_bf16, ta, tb):
            P.barrier()
            A.reset()
            WR = A.alloc([128, 8, 3072], BF16)
            XBs = [A.alloc([128, 8, 512], BF16) for _ in range(2)]
            QKs = [A.alloc([128, 16, 512], BF16) for _ in range(2)]
            VSs = [A.alloc([128, 4, 1024], BF16) for _ in range(2)]
            TRB = [A.alloc([128, 22 * 64], F32) for _ in range(4)]
            tlist = list(range(ta, tb, 512))
            et_done = [0]

            def et_load(h):
                if h < 16:
                    P.dma("sp", TRB[h % 4], trt[li, h], writes=["TRB%d" % (h % 4)])

            def et_exp(h):
                if h < 16:
                    tb_ = TRB[h % 4]
                    P.op("act", lambda e, tb_=tb_: e.activation(out=tb_, in_=tb_, func=AF.Exp),
                         reads=["TRB%d" % (h % 4)], writes=["TRB%d" % (h % 4)])
                    P.dma("sp", ETS[li, h], tb_, reads=["TRB%d" % (h % 4)], writes=[],
                          slot="TRBst%d" % (h % 4))

            def load_xb(ti):
                t0 = tlist[ti]
                xb = XBs[ti % 2]
                if src_bf16 is None:
                    P.dma("pool", xb, fm(x0T)[:, :, t0:t0 + 512], writes=["XB%d" % (ti % 2)])
                else:
                    P.dma("sp", xb, fm(src_bf16)[:, :, t0:t0 + 512], writes=["XB%d" % (ti % 2)])

            load_xb(0)
            wq3 = wqkv[li].rearrange("p (k n) -> p k n", k=8)
            for j in range(6):
                P.dma("pool", WR[:, :, j * 512:(j + 1) * 512], wq3[:, :, j * 512:(j + 1) * 512],
                      writes=["WR%d" % j])
            bank = 0
            for ti, t0 in enumerate(tlist):
                s = ti % 2
                xb, qk, vs = XBs[s], QKs[s], VSs[s]
                xk = "XB%d" % s
                if ti + 1 < len(tlist):
                    load_xb(ti + 1)
                if ti == 0:
                    et_load(0)
                    et_load(1)
                et_load(2 * ti + 2)
                et_load(2 * ti + 3)
                for fo in range(16):
                    if fo == 6:
                        et_exp(2 * ti)
                    if fo == 12:
                        et_exp(2 * ti + 1)
                    b = bank % 8
                    bank += 1
                    for k in range(8):
                        P.mm(ps[b][:, :], WR[:, k, fo * 128:(fo + 1) * 128], xb[:, k, :], k == 0, k == 7,
                             reads=["WR%d" % (fo // 4), xk], writes=[PSK[b]])
                    dst = qk[:, fo, :]
                    sc = 0.125 if fo < 8 else 1.0
                    if fo % 2 == 0:
                        P.op("act", lambda e, dst=dst, b=b, sc=sc: e.activation(
                            out=dst, in_=ps[b][:, :], func=AF.Copy, scale=sc),
                             reads=[PSK[b]], writes=["QK%d_%d" % (s, fo)])
                    else:
                        P.op("dve", lambda e, dst=dst, b=b, sc=sc: e.tensor_scalar_mul(
                            out=dst, in0=ps[b][:, :], scalar1=sc),
                             reads=[PSK[b]], writes=["QK%d_%d" % (s, fo)])
                P.dma("sp", fm(QT)[:, :, t0:t0 + 512], qk[:, 0:8, :],
                      reads=["QK%d_%d" % (s, fo) for fo in range(8)], slot="QKq%d" % s)
                P.dma("sp", fm(KT)[:, :, t0:t0 + 512], qk[:, 8:16, :],
                      reads=["QK%d_%d" % (s, fo) for fo in range(8, 16)], slot="QKk%d" % s)
                for tg in range(4):
                    for hf in range(2):
                        b = bank % 8
                        bank += 1
                        for k in range(8):
                            P.mm(ps[b][:, :], xb[:, k, tg * 128:(tg + 1) * 128],
                                 WR[:, k, 2048 + hf * 512:2048 + (hf + 1) * 512], k == 0, k == 7,
                                 reads=["WR%d" % (4 + hf), xk], writes=[PSK[b]])
                        dst = vs[:, tg, hf * 512:(hf + 1) * 512]
                        if hf == 0:
                            P.op("act", lambda e, dst=dst, b=b: e.copy(dst, ps[b][:, :]),
                                 reads=[PSK[b]], writes=["VS%d_%d_%d" % (s, tg, hf)])
                        else:
                            P.op("dve", lambda e, dst=dst, b=b: e.tensor_copy(dst, ps[b][:, :]),
                                 reads=[PSK[b]], writes=["VS%d_%d_%d" % (s, tg, hf)])
                P.dma("sp", VV[t0:t0 + 512, :].rearrange("(g p) f -> p g f", p=128), vs,
                      reads=["VS%d_%d_%d" % (s, tg, hf) for tg in range(4) for hf in range(2)],
                      slot="VSst%d" % s)

        def att_pass(li, layer, res_src, qs_list, dense_blocks, dst):
            P.barrier()
            A.reset()
            NR = 10
            SRING = [0, 1, 2, 7]
            NSB = 3
            WO = A.alloc([128, 8, 1024], BF16)
            KA = [A.alloc([128, 2, 1024], BF16) for _ in range(3)]
            QA = [A.alloc([128, 2, 512], BF16) for _ in range(3)]
            TR = [A.alloc([128, 2, 22 * 64], F32) for _ in range(3)]
            VB = [A.alloc([128, 8, 1024], BF16) for _ in range(2)]
            SB = [A.alloc([128, 512], F32) for _ in range(NSB)]
            PT = [A.alloc([128, 512], BF16) for _ in range(NR)]
            RZ = [A.alloc([128, 512], F32) for _ in range(2)]
            OT = A.alloc([128, 8, 512], BF16)
            XF = A.alloc([128, 8, 512], F32)
            MSKB = A.alloc([128, 9, 1024], BF16)
            S = ln_alloc(1)
            nb = len(qs_list)
            nunits = nb * 8
            LA = 7
            PF = 2

            def load_unit(u):
                bi, hp = divmod(u, 8)
                qs = qs_list[bi]
                k0 = qs - 256
                s = u % 3
                P.dma("sp", KA[s][0:64, :, :],
                      KT[hp * 128:(hp + 1) * 128, k0:k0 + 1024].rearrange("(hh d) t -> d hh t", d=64),
                      writes=["KA%d" % s])
                P.dma("sp", QA[s][0:64, :, :],
                      QT[hp * 128:(hp + 1) * 128, qs:qs + 512].rearrange("(hh d) t -> d hh t", d=64),
                      writes=["QA%d" % s])
                P.dma("sp", TR[s], ETS[li, 2 * hp:2 * hp + 2].rearrange("h p n -> p h n"),
                      writes=["TR%d" % s])
                for hh in range(2):
                    P.dma("sp", KA[s][64:72, hh, :], MSKB[64:72, bi, :], reads=["MSKB"],
                          writes=["KM%d_%d" % (s, hh)])

            def load_vb(bi):
                k0 = qs_list[bi] - 256
                for q in range(2):
                    P.dma("sp", VB[bi % 2][:, 4 * q:4 * q + 4, :],
                          VV[k0 + 512 * q:k0 + 512 * (q + 1), :].rearrange("(p t) f -> t p f", t=128),
                          writes=["VB%d_%d" % (bi % 2, q)])

            xf_keys = ["XF_%d" % f for f in range(8)]

            def load_xf(bi):
                qs = qs_list[bi]
                for hf in range(2):
                    P.dma("sp", XF[:, 4 * hf:4 * hf + 4, :], fm(res_src)[:, 4 * hf:4 * hf + 4, qs:qs + 512],
                          writes=xf_keys[4 * hf:4 * hf + 4], slot="XFld%d" % hf)

            for s in range(3):
                P.op("pool", lambda e, s=s: e.memset(KA[s][64:128, :, :], 0.0),
                     writes=["KM%d_0" % s, "KM%d_1" % s])
                P.op("pool", lambda e, s=s: e.memset(QA[s][64:128, :, :], 0.0),
                     writes=["QD%d_0" % s, "QD%d_1" % s])
            P.dma("pool", MSKB[64:72, :, :], msk[li].rearrange("b m k -> m b k"), writes=["MSKB"])
            for s in range(3):
                for hh in range(2):
                    P.dma("pool", QA[s][64:72, hh, :], qdl, writes=["QD%d_%d" % (s, hh)], grp="QD")
            load_vb(0)
            issued = [0]

            def issue_units(upto):
                while issued[0] <= upto and issued[0] < nunits:
                    load_unit(issued[0])
                    issued[0] += 1

            issue_units(PF)
            for k in range(8):
                P.dma("pool", WO[:, k, :], wona[li, :, k * 1024:(k + 1) * 1024], writes=["WO%d" % k], grp="WO")
            ybanks = [3, 4]
            S.oz_banks = True
            ycnt = 0
            gidx = 0
            for bi, qs in enumerate(qs_list):
                vbk = ["VB%d_0" % (bi % 2), "VB%d_1" % (bi % 2)]
                vb = VB[bi % 2]
                if bi == 0:
                    load_xf(0)
                its = []
                rng = band_ranges(dense_blocks.get(bi))
                for hp in range(8):
                    for p in (3, 0, 1, 2, 4, 5, 6, 7):
                        for hh in range(2):
                            i0, i1 = rng[p]
                            its.append((hp, hh, p, i0 * 64, (i1 + 1) * 64, 14 - 2 * p + i0, gidx))
                            gidx += 1
                nit = len(its)

                def emit_front(idx):
                    hp, hh, p, c0, c1, n0, g = its[idx]
                    u = bi * 8 + hp
                    s = u % 3
                    si = SRING[g % 4]
                    ri = g % NR
                    sps = ps[si]
                    ka, qa, tr = KA[s], QA[s], TR[s]
                    P.mm(sps[:, c0:c1], ka[0:128, hh, p * 128:(p + 1) * 128], qa[0:128, hh, c0:c1],
                         True, True,
                         reads=["KA%d" % s, "KM%d_%d" % (s, hh), "QA%d" % s, "QD%d_%d" % (s, hh)],
                         writes=[PSK[si]])
                    sbi = g % NSB
                    sb, pt = SB[sbi], PT[ri]
                    trv = tr[:, hh, n0 * 64:n0 * 64 + (c1 - c0)]
                    P.op("act", lambda e: e.activation(out=sb[:, c0:c1], in_=sps[:, c0:c1], func=AF.Exp),
                         reads=[PSK[si]], writes=["SB%d" % sbi])
                    P.op("dve" if g % 2 == 0 else "pool",
                         lambda e: e.tensor_tensor(out=pt[:, c0:c1], in0=sb[:, c0:c1], in1=trv, op=ALU.mult),
                         reads=["SB%d" % sbi, "TR%d" % s], writes=["PT%d" % ri])

                def emit_back_pair(ia, ib, kinds=("pv", "z")):
                    for kind in kinds:
                        for idx in (ia, ib):
                            hp, hh, p, c0, c1, n0, g = its[idx]
                            ri = g % NR
                            pt = PT[ri]
                            st = hp % 2
                            ob, zb = 3 + 2 * st, 4 + 2 * st
                            first = p == 3
                            last = p == 7
                            if kind == "pv":
                                P.mm(ps[ob][hh * 64:(hh + 1) * 64, c0:c1],
                                     vb[:, p, hp * 128 + hh * 64:hp * 128 + (hh + 1) * 64], pt[:, c0:c1],
                                     first, last, reads=[vbk[p // 4], "PT%d" % ri],
                                     writes=["O%d_%d" % (st, hh)], tp=(0, 64 * hh))
                            else:
                                P.mm(ps[zb][hh * 64:(hh + 1) * 64, c0:c1], ONES1[:, :], pt[:, c0:c1],
                                     first, last, reads=["ones1", "PT%d" % ri],
                                     writes=["Z%d_%d" % (st, hh)], tp=(0, 64 * hh))
                    hp, hh, p = its[ib][0:3]
                    if p == 7 and "z" in kinds:
                        def norm_act(hp=hp):
                            st = hp % 2
                            zb = 4 + 2 * st
                            rz = RZ[st]
                            P.op("act", lambda e: e.activation(out=rz, in_=ps[zb][:, :], func=AF.Ln),
                                 reads=["Z%d_0" % st, "Z%d_1" % st], writes=["RZ%d" % st])
                            P.op("act", lambda e: e.activation(out=rz, in_=rz, func=AF.Exp, scale=-1.0),
                                 reads=["RZ%d" % st], writes=["RZ%d" % st])

                        def norm_dve(hp=hp):
                            st = hp % 2
                            ob = 3 + 2 * st
                            rz = RZ[st]
                            otv = OT[:, hp, :]
                            P.op("dve", lambda e: e.tensor_tensor(out=otv, in0=ps[ob][:, :], in1=rz, op=ALU.mult),
                                 reads=["O%d_0" % st, "O%d_1" % st, "RZ%d" % st], writes=["OT%d" % hp])
                        norm_q.append([norm_act, 0, 7])
                        norm_q.append([norm_dve, 0, 10])

                norm_q = []
                for idx in range(nit + LA + 2):
                    for ent in norm_q:
                        ent[1] += 1
                    while norm_q and norm_q[0][1] >= norm_q[0][2]:
                        norm_q.pop(0)[0]()
                    if idx < nit:
                        hp, hh, p = its[idx][0:3]
                        if hh == 0 and p == 0:
                            issue_units(bi * 8 + hp + PF)
                            if hp == 3 and bi + 1 < nb:
                                load_vb(bi + 1)
                        emit_front(idx)
                    j = idx - LA
                    if j >= 1 and j % 2 == 1 and j < nit:
                        emit_back_pair(j - 1, j, kinds=("pv",))
                    j2 = idx - LA - 1
                    if j2 >= 1 and j2 % 2 == 1 and j2 < nit:
                        emit_back_pair(j2 - 1, j2, kinds=("z",))
                    if idx % 8 == 4:
                        drain(S, 1)
                while norm_q:
                    norm_q.pop(0)[0]()
                drain(S)
                for f in range(8):
                    yb = ybanks[ycnt % 2]
                    ycnt += 1
                    for hp in range(8):
                        P.mm(ps[yb][:, :], WO[:, hp, f * 128:(f + 1) * 128], OT[:, hp, :], hp == 0, hp == 7,
                             reads=["WO%d" % hp, "OT%d" % hp], writes=bk(S, yb))
                    epi_a(S, f, ps[yb][:, :], bk(S, yb), XF[:, f, :], xf_keys[f], 5, 7, lag=1)
                nxt = (lambda bi=bi: load_xf(bi + 1)) if bi + 1 < nb else None
                epi_b(S, XF, xf_keys, 5, 7, 0, 1, layer,
                      fm(XS[dst])[:, :, qs:qs + 512], fm(XSb[dst])[:, :, qs:qs + 512], after=nxt,
                      use_pool=False, defer_head=True)
            drain(S)

        def ffn_pass(layer, src, tiles, dst, final_out=False):
            P.barrier()
            A.reset()
            RING = [A.alloc([128, 4, 8, 128], BF16) for _ in range(6)]
            XB = A.alloc([128, 8, 1024], BF16)
            XF = A.alloc([128, 8, 1024], F32)
            AT = A.alloc([128, 22, 1024], BF16)
            SGT = [A.alloc([128, 512], F32) for _ in range(2)]
            S = ln_alloc(2)
            sg = 0
            PFW = 4
            wl = []
            for ti in range(len(tiles)):
                wl += [("i", gq) for gq in range(11)] + [("o", f) for f in range(8)]
            issued = [0]

            def rgo_view(slot):
                return RING[slot].rearrange("p a b c -> p (a b c)")[:, 0:22 * 128].rearrange(
                    "p (c n) -> p c n", c=22)

            def issue_w(upto):
                while issued[0] <= upto and issued[0] < len(wl):
                    i = issued[0]
                    slot = i % 6
                    kind, j = wl[i]
                    if kind == "i":
                        P.dma("pool", RING[slot].rearrange("p a b c -> p (a b c)"), wfi[layer, j],
                              writes=["RING%d" % slot])
                    else:
                        P.dma("pool", rgo_view(slot), wfo[layer, j].rearrange("p (c n) -> p c n", c=22),
                              writes=["RING%d" % slot])
                    issued[0] += 1

            xb_keys = ["XBh0", "XBh1"]
            xf_keys = [["XF_%d_%d" % (nt, f) for f in range(8)] for nt in range(2)]

            def load_xb(ti):
                t0, T = tiles[ti]
                for hf in range(2):
                    P.dma("sp", XB[:, 4 * hf:4 * hf + 4, 0:T], fm(XSb[src])[:, 4 * hf:4 * hf + 4, t0:t0 + T],
                          writes=["XBh%d" % hf])

            def load_xf(ti):
                t0, T = tiles[ti]
                for nt in range(T // 512):
                    for hf in range(2):
                        P.dma("sp", XF[:, 4 * hf:4 * hf + 4, nt * 512:(nt + 1) * 512],
                              fm(XS[src])[:, 4 * hf:4 * hf + 4, t0 + nt * 512:t0 + (nt + 1) * 512],
                              writes=xf_keys[nt][4 * hf:4 * hf + 4], slot="XFld%d_%d" % (nt, hf))

            load_xb(0)
            issue_w(PFW)
            load_xf(0)
            wpos = 0
            for ti, (t0, T) in enumerate(tiles):
                nnt = T // 512
                for gq in range(11):
                    if gq >= 1:
                        drain(S, 2)
                    issue_w(wpos + PFW)
                    slot = wpos % 6
                    wpos += 1
                    rg = RING[slot]
                    rk = "RING%d" % slot
                    for j in range(2):
                        c = 2 * gq + j
                        gb = [0, 1] if c % 2 == 0 else [4, 5]
                        hb = [2, 3] if c % 2 == 0 else [6, 7]
                        for (banks, wsel) in ((gb, 2 * j), (hb, 2 * j + 1)):
                            for k in range(8):
                                for nt in range(nnt):
                                    P.mm(ps[banks[nt]][:, :], rg[:, wsel, k, :], XB[:, k, nt * 512:(nt + 1) * 512],
                                         k == 0, k == 7, reads=[rk, xb_keys[k // 4]], writes=[PSK[banks[nt]]])
                        for nt in range(nnt):
                            sgt = SGT[sg % 2]
                            sgk = "SGT%d" % (sg % 2)
                            sg += 1
                            gps, hps = ps[gb[nt]], ps[hb[nt]]
                            P.op("act", lambda e, sgt=sgt, gps=gps: e.activation(out=sgt, in_=gps[:, :], func=AF.Silu),
                                 reads=[PSK[gb[nt]]], writes=[sgk])
                            atv = AT[:, c, nt * 512:(nt + 1) * 512]
                            P.op("dve", lambda e, atv=atv, hps=hps, sgt=sgt: e.tensor_tensor(
                                out=atv, in0=hps[:, :], in1=sgt, op=ALU.mult),
                                 reads=[PSK[hb[nt]], sgk], writes=["AT%d_%d" % (c, nt)])
                drain(S)
                if ti + 1 < len(tiles):
                    load_xb(ti + 1)
                for f in range(8):
                    issue_w(wpos + PFW)
                    slot = wpos % 6
                    wpos += 1
                    rgo = rgo_view(slot)
                    rk = "RING%d" % slot
                    yb = [0, 1] if f % 2 == 0 else [2, 3]
                    for c in range(22):
                        for nt in range(nnt):
                            P.mm(ps[yb[nt]][:, :], rgo[:, c, :], AT[:, c, nt * 512:(nt + 1) * 512],
                                 c == 0, c == 21, reads=[rk, "AT%d_%d" % (c, nt)], writes=[PSK[yb[nt]]])
                    for nt in range(nnt):
                        epi_a(S, f, ps[yb[nt]][:, :], PSK[yb[nt]], XF[:, f, nt * 512:(nt + 1) * 512],
                              xf_keys[nt][f], 4 + nt, 6 + nt, use_pool=False, lag=2)
                for nt in range(nnt):
                    ta = t0 + nt * 512
                    if final_out:
                        d32 = fm(yT)[:, :, ta - 512:ta]
                        d16 = None
                    else:
                        d32 = fm(XS[dst])[:, :, ta:ta + 512]
                        d16 = fm(XSb[dst])[:, :, ta:ta + 512]
                    nxt = None
                    if nt == nnt - 1 and ti + 1 < len(tiles):
                        nxt = lambda ti=ti: load_xf(ti + 1)
                    epi_b(S, XF[:, :, nt * 512:(nt + 1) * 512], xf_keys[nt], 4 + nt, 6 + nt, 2, 3, layer,
                          d32, d16, final=final_out, use_pool=False, nt=nt, after=nxt, defer_head=True)
            drain(S)

        def sg_pass(li, layer, src, ta, tb, dst):
            P.barrier()
            A.reset()
            WR = A.alloc([128, 8, 3072], BF16)
            XBs = [A.alloc([128, 8, 512], BF16) for _ in range(2)]
            XF = A.alloc([128, 8, 512], F32)
            U = A.alloc([128, 8, 512], F32)
            GT = A.alloc([128, 8, 512], BF16)
            VG = [A.alloc([128, 1024], F32) for _ in range(4)]
            VNB = [A.alloc([128, 1024], BF16) for _ in range(4)]
            BS2 = A.alloc([128, 8, 128], F32)
            WS = A.alloc([128, 16, 128], BF16)
            BS = A.alloc([128, 8, 128], F32)
            MIX = [A.alloc([128, 8, 128], F32) for _ in range(2)]
            ST = A.alloc([128, 4, 12], F32)
            MV = A.alloc([128, 4, 2], F32)
            RS = A.alloc([128, 4], F32)
            NBV = A.alloc([128, 4], F32)
            S = ln_alloc(1)
            tlist = list(range(ta, tb, 512))
            ntl = len(tlist)
            xf_keys = ["XF_%d" % f for f in range(8)]

            def load_xb(ti):
                t0 = tlist[ti]
                P.dma("sp", XBs[ti % 2], fm(XSb[src])[:, :, t0:t0 + 512], writes=["XB%d" % (ti % 2)])

            def load_xf(ti):
                t0 = tlist[ti]
                for hf in range(2):
                    P.dma("sp", XF[:, 4 * hf:4 * hf + 4, :], fm(XS[src])[:, 4 * hf:4 * hf + 4, t0:t0 + 512],
                          writes=xf_keys[4 * hf:4 * hf + 4], slot="XFld%d" % hf)

            load_xb(0)
            wi3 = wsgi[li].rearrange("p (k n) -> p k n", k=8)
            wo3 = wsgo[li].rearrange("p (k n) -> p k n", k=8)
            for j in (2, 3, 0, 1):
                P.dma("pool", WR[:, :, j * 512:(j + 1) * 512], wi3[:, :, j * 512:(j + 1) * 512],
                      writes=["WR%d" % j])
            P.dma("pool", WS.rearrange("p a b -> p (a b)"), wsT[li], writes=["WS"])
            for j in (4, 5):
                P.dma("pool", WR[:, :, j * 512:(j + 1) * 512], wo3[:, :, (j - 4) * 512:(j - 3) * 512],
                      writes=["WR%d" % j])
            P.dma("sp", BS.rearrange("p a b -> p (a b)"), bsb[li], writes=["BS"])
            load_xf(0)
            if ntl > 1:
                load_xb(1)
            bank = [0]
            mcnt = [0]
            ybanks = [5, 2]
            ycnt = [0]

            def v_chunks(ti, tcs):
                XB = XBs[ti % 2]
                xk = "XB%d" % (ti % 2)
                for tc in tcs:
                    vg = VG[tc]
                    for hf in range(2):
                        b = bank[0] % 2
                        bank[0] += 1
                        for k in range(8):
                            P.mm(ps[b][:, :], XB[:, k, tc * 128:(tc + 1) * 128],
                                 WR[:, k, 1024 + hf * 512:1024 + (hf + 1) * 512], k == 0, k == 7,
                                 reads=["WR%d" % (2 + hf), xk], writes=[PSK[b]])
                        vgh = vg[:, hf * 512:(hf + 1) * 512]
                        P.op("act", lambda e, vgh=vgh, b=b: e.activation(out=vgh, in_=ps[b][:, :], func=AF.Gelu),
                             reads=[PSK[b]], writes=["VG%d_%d" % (tc, hf)])
                        sto = ST[:, tc, hf * 6:(hf + 1) * 6]
                        P.op("dve", lambda e, sto=sto, vgh=vgh: e.bn_stats(sto, vgh),
                             reads=["VG%d_%d" % (tc, hf)], writes=["ST%d_%d" % (tc, hf)])
                    P.op("dve", lambda e, tc=tc: e.bn_aggr(MV[:, tc, :], ST[:, tc, :]),
                         reads=["ST%d_0" % tc, "ST%d_1" % tc], writes=["MV%d" % tc])

            mvk = ["MV%d" % tc for tc in range(4)]

            def rstd():
                P.op("dve", lambda e: e.tensor_scalar_add(out=RS[:, :], in0=MV[:, :, 1], scalar1=EPS),
                     reads=mvk, writes=["RS"])
                P.op("act", lambda e: e.activation(out=RS[:, :], in_=RS[:, :], func=AF.Ln),
                     reads=["RS"], writes=["RS"])
                P.op("act", lambda e: e.activation(out=RS[:, :], in_=RS[:, :], func=AF.Exp, scale=-0.5),
                     reads=["RS"], writes=["RS"])
                P.op("dve", lambda e: e.scalar_tensor_tensor(out=NBV[:, :], in0=MV[:, :, 0], scalar=-1.0,
                                                             in1=RS[:, :], op0=ALU.mult, op1=ALU.mult),
                     reads=mvk + ["RS"], writes=["NBV"])

            v_chunks(0, range(4))
            rstd()
            rbanks = [0, 1, 2, 5]
            wsf = WS.rearrange("p a b -> p (a b)")
            for j in range(4):
                P.mm(ps[rbanks[j]][:, :], ONES128[:, :], wsf[:, j * 512:(j + 1) * 512], True, True,
                     reads=["ones128", "WS"], writes=[PSK[rbanks[j]]])
            for fc in range(8):
                for hh in range(2):
                    g = 2 * fc + hh
                    rb = rbanks[g // 4]
                    col = (g % 4) * 128
                    bcol = (li * 2 + 1) * 8 + fc
                    sl = slice(hh * 64, (hh + 1) * 64)
                    P.op("dve", lambda e, sl=sl, rb=rb, col=col, bcol=bcol, fc=fc: e.scalar_tensor_tensor(
                        out=BS2[sl, fc, :], in0=ps[rb][sl, col:col + 128], scalar=SGFM[sl, bcol:bcol + 1],
                        in1=BS[sl, fc, :], op0=ALU.mult, op1=ALU.add),
                         reads=[PSK[rb], "sgfm", "BS"], writes=["BS2_%d_%d" % (fc, hh)])
            bs2_keys = ["BS2_%d_%d" % (fc, hh) for fc in range(8) for hh in range(2)]
            for ti, t0 in enumerate(tlist):
                XB = XBs[ti % 2]
                xk = "XB%d" % (ti % 2)
                for fo in range(8):
                    b = bank[0] % 2
                    bank[0] += 1
                    for k in range(8):
                        P.mm(ps[b][:, :], WR[:, k, fo * 128:(fo + 1) * 128], XB[:, k, :], k == 0, k == 7,
                             reads=["WR%d" % (fo // 4), xk], writes=[PSK[b]])
                    uv = U[:, fo, :]
                    P.op("act", lambda e, uv=uv, b=b: e.activation(out=uv, in_=ps[b][:, :], func=AF.Gelu),
                         reads=[PSK[b]], writes=["U%d" % fo])
                    drain(S, 2 if fo == 0 else 1)
                drain(S)
                for tc in range(4):
                    vg, vnb = VG[tc], VNB[tc]
                    P.op("dve", lambda e, vg=vg, vnb=vnb, tc=tc: e.tensor_scalar(
                        out=vnb, in0=vg, scalar1=RS[:, tc:tc + 1], scalar2=NBV[:, tc:tc + 1],
                        op0=ALU.mult, op1=ALU.add),
                         reads=["VG%d_0" % tc, "VG%d_1" % tc, "RS", "NBV"], writes=["VNB%d" % tc])
                if ti + 1 < ntl:
                    if ti + 2 < ntl:
                        load_xb(ti + 2) if False else None
                    v_chunks(ti + 1, (0, 1))
                for tc in range(4):
                    vnb = VNB[tc]
                    ci = mcnt[0] % 2
                    mcnt[0] += 1
                    for fc in range(8):
                        mb = 3 + fc // 4
                        col = (fc % 4) * 128
                        for hh in range(2):
                            g = 2 * fc + hh
                            P.mm(ps[mb][hh * 64:(hh + 1) * 64, col:col + 128], vnb[:, g * 64:(g + 1) * 64],
                                 WS[:, g, :], True, True, reads=["VNB%d" % tc, "WS"],
                                 writes=["M%d_%d" % (mb, hh)], tp=(0, 64 * hh))
                    mix = MIX[ci]
                    for fc in range(8):
                        mb = 3 + fc // 4
                        col = (fc % 4) * 128
                        gcol = (li * 2) * 8 + fc
                        P.op("dve", lambda e, mix=mix, mb=mb, col=col, gcol=gcol, fc=fc: e.scalar_tensor_tensor(
                            out=mix[:, fc, :], in0=ps[mb][:, col:col + 128], scalar=SGFM[:, gcol:gcol + 1],
                            in1=BS2[:, fc, :], op0=ALU.mult, op1=ALU.add),
                             reads=["M%d_0" % mb, "M%d_1" % mb, "sgfm", "BS2_%d_0" % fc, "BS2_%d_1" % fc],
                             writes=["MIX%d_%d" % (ci, fc)])
                    gv = GT[:, :, tc * 128:(tc + 1) * 128]
                    uvv = U[:, :, tc * 128:(tc + 1) * 128]
                    P.op("pool", lambda e, gv=gv, mix=mix, uvv=uvv: e.tensor_tensor(out=gv, in0=mix, in1=uvv,
                                                                                 op=ALU.mult),
                         reads=["MIX%d_%d" % (ci, fc) for fc in range(8)] + ["U%d" % fo for fo in range(8)],
                         writes=["GT%d" % tc])
                if ti + 1 < ntl:
                    v_chunks(ti + 1, (2, 3))
                gt_keys = ["GT%d" % tc for tc in range(4)]
                for f in range(8):
                    yb = ybanks[ycnt[0] % 2]
                    ycnt[0] += 1
                    for k in range(8):
                        P.mm(ps[yb][:, :], WR[:, k, 2048 + f * 128:2048 + (f + 1) * 128], GT[:, k, :], k == 0, k == 7,
                             reads=["WR%d" % (4 + f // 4)] + gt_keys, writes=[PSK[yb]])
                    epi_a(S, f, ps[yb][:, :], PSK[yb], XF[:, f, :], xf_keys[f], 6, 7, use_pool=False, lag=2)
                nxt = (lambda ti=ti: load_xf(ti + 1)) if ti + 1 < ntl else None
                epi_b(S, XF, xf_keys, 6, 7, 0, 1, layer,
                      fm(XS[dst])[:, :, t0:t0 + 512], fm(XSb[dst])[:, :, t0:t0 + 512], use_pool=False,
                      after=nxt)
                if ti + 1 < ntl:
                    rstd()
                    if ti + 2 < ntl:
                        load_xb(ti + 2)
            drain(S)

        big = [(256 + 1024 * i, 1024) for i in range(4)] + [(4352, 512)]
        small = [(512 + 1024 * i, 1024) for i in range(4)]
        qkv_pass(0, None, 0, NTOK)
        att_pass(0, 0, x0T, [256 + 512 * b for b in range(9)], {0: "s0", 8: "e0"}, 0)
        ffn_pass(0, 0, big, 1)
        sg_pass(0, 1, 1, 256, 4864, 0)
        ffn_pass(1, 0, big, 1)
        qkv_pass(1, XSb[1], 256, 4864)
        att_pass(1, 2, XS[1], [512 + 512 * b for b in range(8)], {0: "s2", 7: "e2"}, 0)
        ffn_pass(2, 0, small, 1)
        sg_pass(1, 3, 1, 512, 4608, 0)
        ffn_pass(3, 0, small, None, final_out=True)
        P.emit()
    return nc


def _core_geom(i):
    if i < 4:
        return 0, None, 64 * i, 256
    j = i - 4
    return 1, j // 2, 64 * (j % 2), 128


def _shared_inputs(na_w_in, na_rpb, na_w_out, sg_w_in, sg_ln_g, sg_ln_b, sg_w_s, sg_b_s, sg_w_out,
                   ln_mix_g, ln_mix_b, ffn_w_in, ffn_w_out, ln_ffn_g, ln_ffn_b):
    f32 = np.float32
    c = np.ascontiguousarray

    def kmaj(w, n):
        L = w.shape[0]
        return c(w.reshape(L, 8, 128, n).transpose(0, 2, 1, 3).reshape(L, 128, 8 * n)).astype(f32)

    sh = {}
    sh["wqkv"] = kmaj(na_w_in, 3072)
    sh["wona"] = kmaj(na_w_out, 1024)
    sh["wsgi"] = kmaj(sg_w_in, 2048)
    sh["wsgo"] = kmaj(sg_w_out, 1024)
    sh["wsT"] = c(sg_w_s.transpose(0, 3, 1, 2).reshape(2, 128, 16 * 128)).astype(f32)
    bs = sg_b_s.reshape(2, 8, 2, 128)
    bs = np.repeat(bs[:, :, :, None, :], 64, axis=3)
    sh["bsb"] = c(bs.reshape(2, 8, 128, 128).transpose(0, 2, 1, 3).reshape(2, 128, 8 * 128)).astype(f32)
    gb = np.stack([sg_ln_g, sg_ln_b], axis=1)
    sh["sgfm"] = c(gb.reshape(2, 2, 8, 128).transpose(3, 0, 1, 2).reshape(128, 32)).astype(f32)
    w = ffn_w_in.reshape(4, 8, 128, 2, 11, 2, 128)
    sh["wfi"] = c(w.transpose(0, 4, 2, 5, 3, 1, 6).reshape(4, 11, 128, 4 * 8 * 128)).astype(f32)
    w = ffn_w_out.reshape(4, 22, 128, 8, 128)
    sh["wfo"] = c(w.transpose(0, 3, 2, 1, 4).reshape(4, 8, 128, 22 * 128)).astype(f32)
    tabs = np.stack([ln_mix_g, ln_mix_b, ln_ffn_g, ln_ffn_b], axis=0)
    sh["lnp"] = c(tabs.reshape(4, 4, 8, 128).transpose(3, 0, 1, 2).reshape(128, 128)).astype(f32)
    kc = np.arange(64)[:, None]
    qc = np.arange(64)[None, :]
    cs = np.clip(qc - 8, 0, 48)
    colvalid = (kc >= cs) & (kc < cs + 16)
    dcol = np.clip(kc - qc + 15, 0, 30)
    trt = np.zeros((2, 16, 128, 22, 64), f32)
    for n in range(22):
        delta = 10 - n
        for half in range(2):
            dr = delta + half
            if abs(dr) <= 7:
                vals = na_rpb[:, :, dr + 7, :][:, :, dcol]
                vals = np.where(colvalid[None, None], vals, f32(NEG))
            else:
                vals = np.broadcast_to(np.where(colvalid, f32(0.0), f32(NEG))[None, None], (2, 16, 64, 64))
            trt[:, :, half * 64:(half + 1) * 64, n, :] = vals
    sh["trt"] = c(trt.reshape(2, 16, 128, 22 * 64))
    qd = np.zeros((8, 8, 64), f32)
    for m in range(8):
        qd[m, m, :] = 1.0
    sh["qdl"] = qd.reshape(8, 512)
    return sh


def _core_inputs(i, x_prompt, x_sample):
    f32 = np.float32
    grp, b, a, rows = _core_geom(i)
    x = x_prompt[0] if grp == 0 else x_sample[b]
    xg = x.reshape(rows, 64, D)
    buf = np.zeros((80, 64, D), f32)
    lo = max(0, a - 8)
    hi = min(rows, a + 72)
    buf[lo - (a - 8):hi - (a - 8)] = xg[lo:hi]
    x0T = np.ascontiguousarray(buf.reshape(NTOK, D).T)
    msk = np.zeros((2, 9, 8, 16, 64), f32)
    for li in range(2):
        nb = 9 if li == 0 else 8
        for bi in range(nb):
            s = (-4 + 8 * bi) if li == 0 else 8 * bi
            for m in range(8):
                ql = s + m
                qr = a + ql
                for n in range(16):
                    kl = s - 4 + n
                    kr = a + kl
                    if 0 <= qr < rows:
                        rs = min(max(qr - 4, 0), rows - 8)
                        valid = rs <= kr < rs + 8
                    else:
                        valid = -4 <= kl - ql <= 3
                    if not valid:
                        msk[li, bi, m, n, :] = NEG
                    else:
                        kind = {(0, 0): "s0", (0, 8): "e0", (1, 0): "s2", (1, 7): "e2"}.get((li, bi))
                        i0, i1 = band_ranges(kind)[n // 2]
                        assert i0 <= m <= i1, ("window outside computed band", i, li, bi, m, n)
    return {"x0T": x0T, "msk": msk.reshape(2, 9, 8, 1024)}


_NC_CACHE = {}


def kernel(x_prompt, x_sample, na_w_in, na_rpb, na_w_out, sg_w_in, sg_ln_g, sg_ln_b,
           sg_w_s, sg_b_s, sg_w_out, ln_mix_g, ln_mix_b, ffn_w_in, ffn_w_out, ln_ffn_g, ln_ffn_b):
    args = [np.asarray(v, dtype=np.float32) for v in (
        x_prompt, x_sample, na_w_in, na_rpb, na_w_out, sg_w_in, sg_ln_g, sg_ln_b,
        sg_w_s, sg_b_s, sg_w_out, ln_mix_g, ln_mix_b, ffn_w_in, ffn_w_out, ln_ffn_g, ln_ffn_b)]
    x_prompt, x_sample = args[0], args[1]
    shared = _shared_inputs(*args[2:])
    in_maps = []
    for i in range(8):
        m = dict(shared)
        m.update(_core_inputs(i, x_prompt, x_sample))
        in_maps.append(m)
    if "nc" not in _NC_CACHE:
        _NC_CACHE["nc"] = build_program()
    nc = _NC_CACHE["nc"]
    res = run_bass_kernel_spmd(nc, in_maps, core_ids=list(range(8)))
    y_prompt = np.zeros((1, 16384, D), np.float32)
    y_sample = np.zeros((2, 8192, D), np.float32)
    for i in range(8):
        grp, b, a, rows = _core_geom(i)
        yt = np.asarray(res.results[i]["yT"], dtype=np.float32)
        blk = yt.T
        if grp == 0:
            y_prompt[0, a * 64:a * 64 + 4096] = blk
        else:
            y_sample[b, a * 64:a * 64 + 4096] = blk
    return (y_prompt, y_sample)
```

```python
import numpy as np
from contextlib import ExitStack
import concourse.bass as bass
import concourse.mybir as mybir
from concourse.bass_utils import run_bass_kernel_spmd

F32 = mybir.dt.float32
BF16 = mybir.dt.bfloat16
AF = mybir.ActivationFunctionType
ALU = mybir.AluOpType

D = 1024
NTOK = 5120
ALPHA = float(8.0 ** 0.25)
EPS = 1e-5
NEG = -30000.0
ENGS = ["pe", "act", "dve", "pool", "sp"]
ARENA_BYTES = 189 * 1024


class Op:
    __slots__ = ("eng", "fn", "dma", "slot", "deps", "needs_inc", "sem", "val", "grp")

    def __init__(self, eng, fn, dma, slot):
        self.eng = eng
        self.fn = fn
        self.dma = dma
        self.slot = slot
        self.grp = None
        self.deps = []
        self.needs_inc = False
        self.sem = None
        self.val = None


class Prog:
    def __init__(self, nc):
        self.nc = nc
        self.ops = []
        self.last_writer = {}
        self.readers = {}
        self.final_dmas = []
        self.last_eng = {}
        self.last_slot = {}
        self.bar = None
        self.bar_done = set()

    def barrier(self):
        deps = list(self.last_eng.values()) + list(self.last_slot.values())
        self.bar = deps
        self.bar_done = set()
        self.last_writer = {}
        self.readers = {}

    def op(self, eng, fn, reads=(), writes=(), dma=False, slot=None, final=False):
        o = Op(eng, fn, dma, slot)
        deps = set()
        for r in reads:
            w = self.last_writer.get(r)
            if w is not None:
                deps.add(w)
        for r in writes:
            w = self.last_writer.get(r)
            if w is not None:
                deps.add(w)
            for rd in self.readers.get(r, ()):
                deps.add(rd)
        if self.bar is not None and eng not in self.bar_done:
            self.bar_done.add(eng)
            for d in self.bar:
                deps.add(d)
        for d in deps:
            if (not o.dma) and (not d.dma) and o.eng == "pe" and d.eng == "pe":
                continue
            o.deps.append(d)
        for r in writes:
            self.last_writer[r] = o
            self.readers[r] = []
        for r in reads:
            self.readers.setdefault(r, []).append(o)
        self.ops.append(o)
        if dma:
            self.last_slot[slot] = o
        else:
            self.last_eng[eng] = o
        if final:
            self.final_dmas.append(o)
        return o

    def dma(self, q, out, in_, reads=(), writes=(), slot=None, final=False, grp=None):
        if grp is not None:
            slot = grp
        elif slot is None:
            slot = writes[0] if writes else reads[0]
        slot = q + ":" + slot
        o = self.op(q, lambda e, out=out, in_=in_: e.dma_start(out=out, in_=in_),
                    reads=reads, writes=writes, dma=True, slot=slot, final=final)
        o.grp = grp
        return o

    def mm(self, out, lhsT, rhs, start, stop, reads, writes, tp=None):
        if tp is None:
            fn = lambda e: e.matmul(out, lhsT, rhs, start=start, stop=stop)
        else:
            fn = lambda e: e.matmul(out, lhsT, rhs, start=start, stop=stop, tile_position=tp)
        return self.op("pe", fn, reads=reads, writes=writes)

    def emit(self):
        nc = self.nc
        for o in self.ops:
            for d in o.deps:
                d.needs_inc = True
        for o in self.final_dmas:
            o.needs_inc = True
        slots = []
        seen = set()
        for o in self.ops:
            if o.dma and o.needs_inc and o.slot not in seen:
                seen.add(o.slot)
                slots.append(o.slot)
        self.n_slots = len(slots)
        with ExitStack() as es:
            esem = {e: es.enter_context(nc.semaphore("e_" + e)) for e in ENGS}
            ssem = {s: es.enter_context(nc.semaphore("d%d" % i)) for i, s in enumerate(slots)}
            ecnt = {e: 0 for e in ENGS}
            scnt = {s: 0 for s in slots}
            for o in self.ops:
                if o.dma and o.grp is not None and o.slot in ssem:
                    o.needs_inc = True
                if not o.needs_inc:
                    continue
                if o.dma:
                    scnt[o.slot] += 16
                    o.sem = ssem[o.slot]
                    o.val = scnt[o.slot]
                else:
                    ecnt[o.eng] += 1
                    o.sem = esem[o.eng]
                    o.val = ecnt[o.eng]
            run = []
            for o in self.ops + [None]:
                if o is not None and o.dma and o.grp is not None and o.needs_inc:
                    if run and run[-1].slot != o.slot:
                        for r in run:
                            r.val = run[-1].val
                        run = []
                    run.append(o)
                elif o is None or (o.dma and o.needs_inc):
                    for r in run:
                        r.val = run[-1].val
                    run = []
            block = es.enter_context(nc.Block())
            per = {e: [o for o in self.ops if o.eng == e] for e in ENGS}
            engobj = {"pe": nc.tensor, "act": nc.scalar, "dve": nc.vector, "pool": nc.gpsimd,
                      "sp": nc.sync}
            finals = self.final_dmas

            def make(e):
                def body(h):
                    waited = {}
                    for o in per[e]:
                        need = {}
                        for d in o.deps:
                            k = id(d.sem)
                            if k not in need or need[k][1] < d.val:
                                need[k] = (d.sem, d.val)
                        for k, (sem, val) in need.items():
                            if waited.get(k, 0) >= val:
                                continue
                            h.wait_ge(sem, val)
                            waited[k] = val
                        ins = o.fn(engobj[e])
                        if o.needs_inc:
                            ins.then_inc(o.sem, 16 if o.dma else 1)
                    if e == "sp":
                        for o in finals:
                            h.wait_ge(o.sem, o.val)
                return body

            block.tensor(make("pe"))
            block.scalar(make("act"))
            block.vector(make("dve"))
            block.gpsimd(make("pool"))
            block.sync(make("sp"))


def band_ranges(kind):
    r = [(max(0, 2 * p - 7), min(7, 2 * p + 1)) for p in range(8)]
    ext = {
        None: {},
        "s0": {6: (4, 7), 7: (4, 7)},
        "e0": {0: (0, 3)},
        "s2": {4: (0, 7), 5: (0, 7)},
        "e2": {2: (0, 7)},
    }[kind]
    for p, v in ext.items():
        r[p] = (min(r[p][0], v[0]), max(r[p][1], v[1]))
    return r


class Arena:
    def __init__(self, t):
        self.t = t
        self.off = 0

    def reset(self):
        self.off = 0

    def alloc(self, shape, dt):
        esz = 4 if dt == F32 else 2
        n = 1
        for s in shape[1:]:
            n *= s
        nbytes = (n * esz + 63) // 64 * 64
        assert self.off + nbytes <= ARENA_BYTES, ("arena overflow", self.off + nbytes)
        v = self.t[:, self.off // 2:(self.off + n * esz) // 2]
        self.off += nbytes
        if dt == F32:
            v = v.bitcast(F32)
        if len(shape) == 3:
            v = v.rearrange("p (a b) -> p a b", a=shape[1])
        elif len(shape) == 4:
            v = v.rearrange("p (a b c) -> p a b c", a=shape[1], b=shape[2])
        return v


def build_program():
    nc = bass.Bass("TRN2", target_bir_lowering=False)

    def din(name, shape):
        return nc.dram_tensor(name, shape, F32, kind="ExternalInput").ap()

    def dscr(name, shape, dt):
        return nc.dram_tensor(name, shape, dt, kind="Internal").ap()

    x0T = din("x0T", [D, NTOK])
    wqkv = din("wqkv", [2, 128, 8 * 3072])
    wona = din("wona", [2, 128, 8 * 1024])
    wsgi = din("wsgi", [2, 128, 8 * 2048])
    wsgo = din("wsgo", [2, 128, 8 * 1024])
    wsT = din("wsT", [2, 128, 16 * 128])
    bsb = din("bsb", [2, 128, 8 * 128])
    sgfm_d = din("sgfm", [128, 32])
    wfi = din("wfi", [4, 11, 128, 4 * 8 * 128])
    wfo = din("wfo", [4, 8, 128, 22 * 128])
    lnp_d = din("lnp", [128, 4 * 4 * 8])
    trt = din("trt", [2, 16, 128, 22 * 64])
    msk = din("msk", [2, 9, 8, 1024])
    qdl = din("qdl", [8, 512])
    yT = nc.dram_tensor("yT", [D, 4096], F32, kind="ExternalOutput").ap()

    XS = [dscr("xs0", [D, NTOK], F32), dscr("xs1", [D, NTOK], F32)]
    XSb = [dscr("xsb0", [D, NTOK], BF16), dscr("xsb1", [D, NTOK], BF16)]
    QT = dscr("qt", [D, NTOK], BF16)
    KT = dscr("kt", [D, NTOK], BF16)
    VV = dscr("vv", [NTOK, D], BF16)
    ETS = dscr("ets", [2, 16, 128, 22 * 64], F32)

    def fm(ap):
        return ap.rearrange("(f p) t -> p f t", p=128)

    with ExitStack() as es:
        arena_t = es.enter_context(nc.sbuf_tensor("arena", [128, ARENA_BYTES // 2], BF16))
        ONESM = es.enter_context(nc.sbuf_tensor("onesm", [128, 128], BF16))
        ONES1 = es.enter_context(nc.sbuf_tensor("ones1", [128, 64], BF16))
        LNP = es.enter_context(nc.sbuf_tensor("lnp_sb", [128, 128], F32))
        SGFM = es.enter_context(nc.sbuf_tensor("sgfm_sb", [128, 32], F32))
        ONES128 = es.enter_context(nc.sbuf_tensor("ones128", [128, 128], BF16))
        ps = [es.enter_context(nc.psum_tensor("ps%d" % i, [128, 512], F32)) for i in range(8)]
        PSK = ["ps%d" % i for i in range(8)]
        A = Arena(arena_t)
        P = Prog(nc)

        P.op("pool", lambda e: e.memset(ONESM[:], 1.0 / 1024.0), writes=["onesm"])
        P.op("pool", lambda e: e.memset(ONES1[:], 1.0), writes=["ones1"])
        P.dma("sp", LNP[:], lnp_d, writes=["lnp"])
        P.dma("sp", SGFM[:], sgfm_d, writes=["sgfm"])
        P.op("pool", lambda e: e.memset(ONES128[:], 1.0), writes=["ones128"])

        def lnp_col(tab, l, f):
            c = (tab * 4 + l) * 8 + f
            return LNP[:, c:c + 1]

        class LNState:
            pass

        def ln_alloc(n_nt):
            S = LNState()
            S.sbt = [A.alloc([128, 512], BF16) for _ in range(4)]
            S.s2t = [A.alloc([128, 512], BF16) for _ in range(4)]
            S.ms = [A.alloc([128, 512], F32) for _ in range(n_nt)]
            S.m2 = A.alloc([128, 512], F32)
            S.var = [A.alloc([128, 512], F32) for _ in range(n_nt)]
            S.a = [A.alloc([128, 512], F32) for _ in range(n_nt)]
            S.b = [A.alloc([128, 512], F32) for _ in range(n_nt)]
            S.t1 = [A.alloc([128, 512], F32) for _ in range(2)]
            S.cnt = 0
            S.tcnt = 0
            S.pending = []
            S.tail = []
            return S

        def drain(S, n=None):
            while S.tail and (n is None or n > 0):
                S.tail.pop(0)[1]()
                if n is not None:
                    n -= 1

        BANK_KEYS = {3: ["O0_0", "O0_1"], 4: ["Z0_0", "Z0_1"], 5: ["O1_0", "O1_1"], 6: ["Z1_0", "Z1_1"]}

        def bk(S, b):
            return [PSK[b]] + (BANK_KEYS.get(b, []) if getattr(S, "oz_banks", False) else [])

        def epi_stats(S, ent):
            f, i, mean_b, ex2_b = ent
            P.mm(ps[mean_b][:, :], ONESM[:, :], S.sbt[i], f == 0, f == 7,
                 reads=["sbt%d" % i, "onesm"], writes=bk(S, mean_b))
            P.mm(ps[ex2_b][:, :], ONESM[:, :], S.s2t[i], f == 0, f == 7,
                 reads=["s2t%d" % i, "onesm"], writes=bk(S, ex2_b))

        def epi_a(S, f, y_ps, y_key, xf_f, xf_key, mean_b, ex2_b, use_pool=True, lag=2):
            i = S.cnt % 4
            S.cnt += 1
            P.op("dve", lambda e: e.scalar_tensor_tensor(out=xf_f, in0=xf_f, scalar=ALPHA, in1=y_ps,
                                                         op0=ALU.mult, op1=ALU.add),
                 reads=(y_key if isinstance(y_key, list) else [y_key]) + [xf_key], writes=[xf_key])
            sbt, s2t = S.sbt[i], S.s2t[i]
            P.op("act", lambda e: e.copy(sbt, xf_f), reads=[xf_key], writes=["sbt%d" % i])
            if use_pool:
                P.op("pool", lambda e: e.tensor_tensor(out=s2t, in0=xf_f, in1=xf_f, op=ALU.mult),
                     reads=[xf_key], writes=["s2t%d" % i])
            else:
                P.op("act", lambda e: e.activation(out=s2t, in_=xf_f, func=AF.Square),
                     reads=[xf_key], writes=["s2t%d" % i])
            S.pending.append((f, i, mean_b, ex2_b))
            while len(S.pending) > lag:
                epi_stats(S, S.pending.pop(0))

        def epi_flush(S):
            while S.pending:
                epi_stats(S, S.pending.pop(0))

        def epi_b(S, xf, xf_keys, mean_b, ex2_b, gtab, btab, l, dst_f32, dst_bf16, final=False,
                  use_pool=True, nt=0, after=None, defer_head=False):
            epi_flush(S)
            ms, sa, sb_, var = S.ms[nt], S.a[nt], S.b[nt], S.var[nt]
            kms, ka_, kb_, kv = "ms%d" % nt, "lna%d" % nt, "lnb%d" % nt, "var%d" % nt
            P.op("act", lambda e: e.copy(ms, ps[mean_b][:, :]), reads=bk(S, mean_b), writes=[kms])
            P.op("pool" if use_pool else "dve",
                 lambda e: e.tensor_tensor(out=S.m2, in0=ms, in1=ms, op=ALU.mult),
                 reads=[kms], writes=["m2"])
            P.op("dve", lambda e: e.tensor_tensor(out=var, in0=ps[ex2_b][:, :], in1=S.m2,
                                                  op=ALU.subtract),
                 reads=bk(S, ex2_b) + ["m2"], writes=[kv])

            def head2():
                P.op("dve", lambda e: e.tensor_scalar_add(out=var, in0=var, scalar1=EPS),
                     reads=[kv], writes=[kv])
                P.op("act", lambda e: e.activation(out=var, in_=var, func=AF.Ln),
                     reads=[kv], writes=[kv])
                P.op("act", lambda e: e.activation(out=sa, in_=var, func=AF.Exp, scale=-0.5),
                     reads=[kv], writes=[ka_])
                P.op("dve", lambda e: e.scalar_tensor_tensor(out=sb_, in0=ms, scalar=-1.0, in1=sa,
                                                             op0=ALU.mult, op1=ALU.mult),
                     reads=[kms, ka_], writes=[kb_])

            if defer_head:
                k = 0
                while k < len(S.tail) and S.tail[k][0] == "h":
                    k += 1
                S.tail.insert(k, ("h", head2))
            else:
                head2()

            tis = {}

            def dve_part(f):
                ti = S.tcnt % 2
                S.tcnt += 1
                tis[f] = ti
                t1 = S.t1[ti]
                tk = "t1_%d" % ti
                xf_f = xf[:, f, :]
                eng = "dve" if (f % 2 == 0 or not use_pool) else "pool"
                P.op(eng, lambda e: e.tensor_tensor(out=t1, in0=xf_f, in1=sa, op=ALU.mult),
                     reads=[xf_keys[f], ka_], writes=[tk])
                P.op(eng, lambda e: e.tensor_tensor(out=t1, in0=t1, in1=sb_, op=ALU.add),
                     reads=[tk, kb_], writes=[tk])

            def act_part(f):
                ti = tis[f]
                t1 = S.t1[ti]
                tk = "t1_%d" % ti
                xf_f = xf[:, f, :]
                g = lnp_col(gtab, l, f)
                b = lnp_col(btab, l, f)
                P.op("act", lambda e: e.activation(out=xf_f, in_=t1, func=AF.Identity, bias=b, scale=g),
                     reads=[tk, "lnp"], writes=[xf_keys[f]])

            def apply_f(f):
                dve_part(f)
                if f >= 1:
                    act_part(f - 1)

            def stores():
                act_part(7)
                for hf in range(2):
                    P.dma("sp", dst_f32[:, 4 * hf:4 * hf + 4, :], xf[:, 4 * hf:4 * hf + 4, :],
                          reads=xf_keys[4 * hf:4 * hf + 4], writes=[], slot=xf_keys[4 * hf] + "_st",
                          final=final)
                if dst_bf16 is not None:
                    for hf in range(2):
                        P.dma("pool", dst_bf16[:, 4 * hf:4 * hf + 4, :], xf[:, 4 * hf:4 * hf + 4, :],
                              reads=xf_keys[4 * hf:4 * hf + 4], writes=[], slot=xf_keys[4 * hf] + "_sb")
                if after is not None:
                    after()

            for f in range(8):
                S.tail.append(("a", lambda f=f: apply_f(f)))
            S.tail.append(("s", stores))

        def qkv_pass(li, src_bf16, ta, tb):
            P.barrier()
            A.reset()
            WR = A.alloc([128, 8, 3072], BF16)
            XBs = [A.alloc([128, 8, 512], BF16) for _ in range(2)]
            QKs = [A.alloc([128, 16, 512], BF16) for _ in range(2)]
            VSs = [A.alloc([128, 4, 1024], BF16) for _ in range(2)]
            TRB = [A.alloc([128, 22 * 64], F32) for _ in range(4)]
            tlist = list(range(ta, tb, 512))
            et_done = [0]

            def et_load(h):
                if h < 16:
                    P.dma("sp", TRB[h % 4], trt[li, h], writes=["TRB%d" % (h % 4)])

            def et_exp(h):
                if h < 16:
                    tb_ = TRB[h % 4]
                    P.op("act", lambda e, tb_=tb_: e.activation(out=tb_, in_=tb_, func=AF.Exp),
                         reads=["TRB%d" % (h % 4)], writes=["TRB%d" % (h % 4)])
                    P.dma("sp", ETS[li, h], tb_, reads=["TRB%d" % (h % 4)], writes=[],
                          slot="TRBst%d" % (h % 4))

            def load_xb(ti):
                t0 = tlist[ti]
                xb = XBs[ti % 2]
                if src_bf16 is None:
                    P.dma("pool", xb, fm(x0T)[:, :, t0:t0 + 512], writes=["XB%d" % (ti % 2)])
                else:
                    P.dma("sp", xb, fm(src_bf16)[:, :, t0:t0 + 512], writes=["XB%d" % (ti % 2)])

            load_xb(0)
            wq3 = wqkv[li].rearrange("p (k n) -> p k n", k=8)
            for j in range(6):
                P.dma("pool", WR[:, :, j * 512:(j + 1) * 512], wq3[:, :, j * 512:(j + 1) * 512],
                      writes=["WR%d" % j])
            bank = 0
            for ti, t0 in enumerate(tlist):
                s = ti % 2
                xb, qk, vs = XBs[s], QKs[s], VSs[s]
                xk = "XB%d" % s
                if ti + 1 < len(tlist):
                    load_xb(ti + 1)
                if ti == 0:
                    et_load(0)
                    et_load(1)
                et_load(2 * ti + 2)
                et_load(2 * ti + 3)
                for fo in range(16):
                    if fo == 6:
                        et_exp(2 * ti)
                    if fo == 12:
                        et_exp(2 * ti + 1)
                    b = bank % 8
                    bank += 1
                    for k in range(8):
                        P.mm(ps[b][:, :], WR[:, k, fo * 128:(fo + 1) * 128], xb[:, k, :], k == 0, k == 7,
                             reads=["WR%d" % (fo // 4), xk], writes=[PSK[b]])
                    dst = qk[:, fo, :]
                    sc = 0.125 if fo < 8 else 1.0
                    if fo % 2 == 0:
                        P.op("act", lambda e, dst=dst, b=b, sc=sc: e.activation(
                            out=dst, in_=ps[b][:, :], func=AF.Copy, scale=sc),
                             reads=[PSK[b]], writes=["QK%d_%d" % (s, fo)])
                    else:
                        P.op("dve", lambda e, dst=dst, b=b, sc=sc: e.tensor_scalar_mul(
                            out=dst, in0=ps[b][:, :], scalar1=sc),
                             reads=[PSK[b]], writes=["QK%d_%d" % (s, fo)])
                P.dma("sp", fm(QT)[:, :, t0:t0 + 512], qk[:, 0:8, :],
                      reads=["QK%d_%d" % (s, fo) for fo in range(8)], slot="QKq%d" % s)
                P.dma("sp", fm(KT)[:, :, t0:t0 + 512], qk[:, 8:16, :],
                      reads=["QK%d_%d" % (s, fo) for fo in range(8, 16)], slot="QKk%d" % s)
                for tg in range(4):
                    for hf in range(2):
                        b = bank % 8
                        bank += 1
                        for k in range(8):
                            P.mm(ps[b][:, :], xb[:, k, tg * 128:(tg + 1) * 128],
                                 WR[:, k, 2048 + hf * 512:2048 + (hf + 1) * 512], k == 0, k == 7,
                                 reads=["WR%d" % (4 + hf), xk], writes=[PSK[b]])
                        dst = vs[:, tg, hf * 512:(hf + 1) * 512]
                        if hf == 0:
                            P.op("act", lambda e, dst=dst, b=b: e.copy(dst, ps[b][:, :]),
                                 reads=[PSK[b]], writes=["VS%d_%d_%d" % (s, tg, hf)])
                        else:
                            P.op("dve", lambda e, dst=dst, b=b: e.tensor_copy(dst, ps[b][:, :]),
                                 reads=[PSK[b]], writes=["VS%d_%d_%d" % (s, tg, hf)])
                P.dma("sp", VV[t0:t0 + 512, :].rearrange("(g p) f -> p g f", p=128), vs,
                      reads=["VS%d_%d_%d" % (s, tg, hf) for tg in range(4) for hf in range(2)],
                      slot="VSst%d" % s)

        def att_pass(li, layer, res_src, qs_list, dense_blocks, dst):
            P.barrier()
            A.reset()
            NR = 10
            SRING = [0, 1, 2, 7]
            NSB = 3
            WO = A.alloc([128, 8, 1024], BF16)
            KA = [A.alloc([128, 2, 1024], BF16) for _ in range(3)]
            QA = [A.alloc([128, 2, 512], BF16) for _ in range(3)]
            TR = [A.alloc([128, 2, 22 * 64], F32) for _ in range(3)]
            VB = [A.alloc([128, 8, 1024], BF16) for _ in range(2)]
            SB = [A.alloc([128, 512], F32) for _ in range(NSB)]
            PT = [A.alloc([128, 512], BF16) for _ in range(NR)]
            RZ = [A.alloc([128, 512], F32) for _ in range(2)]
            OT = A.alloc([128, 8, 512], BF16)
            XF = A.alloc([128, 8, 512], F32)
            MSKB = A.alloc([128, 9, 1024], BF16)
            S = ln_alloc(1)
            nb = len(qs_list)
            nunits = nb * 8
            LA = 7
            PF = 2

            def load_unit(u):
                bi, hp = divmod(u, 8)
                qs = qs_list[bi]
                k0 = qs - 256
                s = u % 3
                P.dma("sp", KA[s][0:64, :, :],
                      KT[hp * 128:(hp + 1) * 128, k0:k0 + 1024].rearrange("(hh d) t -> d hh t", d=64),
                      writes=["KA%d" % s])
                P.dma("sp", QA[s][0:64, :, :],
                      QT[hp * 128:(hp + 1) * 128, qs:qs + 512].rearrange("(hh d) t -> d hh t", d=64),
                      writes=["QA%d" % s])
                P.dma("sp", TR[s], ETS[li, 2 * hp:2 * hp + 2].rearrange("h p n -> p h n"),
                      writes=["TR%d" % s])
                for hh in range(2):
                    P.dma("sp", KA[s][64:72, hh, :], MSKB[64:72, bi, :], reads=["MSKB"],
                          writes=["KM%d_%d" % (s, hh)])

            def load_vb(bi):
                k0 = qs_list[bi] - 256
                for q in range(2):
                    P.dma("sp", VB[bi % 2][:, 4 * q:4 * q + 4, :],
                          VV[k0 + 512 * q:k0 + 512 * (q + 1), :].rearrange("(p t) f -> t p f", t=128),
                          writes=["VB%d_%d" % (bi % 2, q)])

            xf_keys = ["XF_%d" % f for f in range(8)]

            def load_xf(bi):
                qs = qs_list[bi]
                for hf in range(2):
                    P.dma("sp", XF[:, 4 * hf:4 * hf + 4, :], fm(res_src)[:, 4 * hf:4 * hf + 4, qs:qs + 512],
                          writes=xf_keys[4 * hf:4 * hf + 4], slot="XFld%d" % hf)

            for s in range(3):
                P.op("pool", lambda e, s=s: e.memset(KA[s][64:128, :, :], 0.0),
                     writes=["KM%d_0" % s, "KM%d_1" % s])
                P.op("pool", lambda e, s=s: e.memset(QA[s][64:128, :, :], 0.0),
                     writes=["QD%d_0" % s, "QD%d_1" % s])
            P.dma("pool", MSKB[64:72, :, :], msk[li].rearrange("b m k -> m b k"), writes=["MSKB"])
            for s in range(3):
                for hh in range(2):
                    P.dma("pool", QA[s][64:72, hh, :], qdl, writes=["QD%d_%d" % (s, hh)], grp="QD")
            load_vb(0)
            issued = [0]

            def issue_units(upto):
                while issued[0] <= upto and issued[0] < nunits:
                    load_unit(issued[0])
                    issued[0] += 1

            issue_units(PF)
            for k in range(8):
                P.dma("pool", WO[:, k, :], wona[li, :, k * 1024:(k + 1) * 1024], writes=["WO%d" % k], grp="WO")
            ybanks = [3, 4]
            S.oz_banks = True
            ycnt = 0
            gidx = 0
            for bi, qs in enumerate(qs_list):
                vbk = ["VB%d_0" % (bi % 2), "VB%d_1" % (bi % 2)]
                vb = VB[bi % 2]
                if bi == 0:
                    load_xf(0)
                its = []
                rng = band_ranges(dense_blocks.get(bi))
                for hp in range(8):
                    for p in (3, 0, 1, 2, 4, 5, 6, 7):
                        for hh in range(2):
                            i0, i1 = rng[p]
                            its.append((hp, hh, p, i0 * 64, (i1 + 1) * 64, 14 - 2 * p + i0, gidx))
                            gidx += 1
                nit = len(its)

                def emit_front(idx):
                    hp, hh, p, c0, c1, n0, g = its[idx]
                    u = bi * 8 + hp
                    s = u % 3
                    si = SRING[g % 4]
                    ri = g % NR
                    sps = ps[si]
                    ka, qa, tr = KA[s], QA[s], TR[s]
                    P.mm(sps[:, c0:c1], ka[0:128, hh, p * 128:(p + 1) * 128], qa[0:128, hh, c0:c1],
                         True, True,
                         reads=["KA%d" % s, "KM%d_%d" % (s, hh), "QA%d" % s, "QD%d_%d" % (s, hh)],
                         writes=[PSK[si]])
                    sbi = g % NSB
                    sb, pt = SB[sbi], PT[ri]
                    trv = tr[:, hh, n0 * 64:n0 * 64 + (c1 - c0)]
                    P.op("act", lambda e: e.activation(out=sb[:, c0:c1], in_=sps[:, c0:c1], func=AF.Exp),
                         reads=[PSK[si]], writes=["SB%d" % sbi])
                    P.op("dve" if g % 2 == 0 else "pool",
                         lambda e: e.tensor_tensor(out=pt[:, c0:c1], in0=sb[:, c0:c1], in1=trv, op=ALU.mult),
                         reads=["SB%d" % sbi, "TR%d" % s], writes=["PT%d" % ri])

                def emit_back_pair(ia, ib):
                    for kind in ("pv", "z"):
                        for idx in (ia, ib):
                            hp, hh, p, c0, c1, n0, g = its[idx]
                            ri = g % NR
                            pt = PT[ri]
                            st = hp % 2
                            ob, zb = 3 + 2 * st, 4 + 2 * st
                            first = p == 3
                            last = p == 7
                            if kind == "pv":
                                P.mm(ps[ob][hh * 64:(hh + 1) * 64, c0:c1],
                                     vb[:, p, hp * 128 + hh * 64:hp * 128 + (hh + 1) * 64], pt[:, c0:c1],
                                     first, last, reads=[vbk[p // 4], "PT%d" % ri],
                                     writes=["O%d_%d" % (st, hh)], tp=(0, 64 * hh))
                            else:
                                P.mm(ps[zb][hh * 64:(hh + 1) * 64, c0:c1], ONES1[:, :], pt[:, c0:c1],
                                     first, last, reads=["ones1", "PT%d" % ri],
                                     writes=["Z%d_%d" % (st, hh)], tp=(0, 64 * hh))
                    hp, hh, p = its[ib][0:3]
                    if p == 7:
                        def norm_act(hp=hp):
                            st = hp % 2
                            zb = 4 + 2 * st
                            rz = RZ[st]
                            P.op("act", lambda e: e.activation(out=rz, in_=ps[zb][:, :], func=AF.Ln),
                                 reads=["Z%d_0" % st, "Z%d_1" % st], writes=["RZ%d" % st])
                            P.op("act", lambda e: e.activation(out=rz, in_=rz, func=AF.Exp, scale=-1.0),
                                 reads=["RZ%d" % st], writes=["RZ%d" % st])

                        def norm_dve(hp=hp):
                            st = hp % 2
                            ob = 3 + 2 * st
                            rz = RZ[st]
                            otv = OT[:, hp, :]
                            P.op("dve", lambda e: e.tensor_tensor(out=otv, in0=ps[ob][:, :], in1=rz, op=ALU.mult),
                                 reads=["O%d_0" % st, "O%d_1" % st, "RZ%d" % st], writes=["OT%d" % hp])
                        norm_q.append([norm_act, 0, 7])
                        norm_q.append([norm_dve, 0, 10])

                norm_q = []
                for idx in range(nit + LA + 1):
                    for ent in norm_q:
                        ent[1] += 1
                    while norm_q and norm_q[0][1] >= norm_q[0][2]:
                        norm_q.pop(0)[0]()
                    if idx < nit:
                        hp, hh, p = its[idx][0:3]
                        if hh == 0 and p == 0:
                            issue_units(bi * 8 + hp + PF)
                            if hp == 3 and bi + 1 < nb:
                                load_vb(bi + 1)
                        emit_front(idx)
                    j = idx - LA
                    if j >= 1 and j % 2 == 1 and j < nit:
                        emit_back_pair(j - 1, j)
                    if idx % 8 == 4:
                        drain(S, 1)
                while norm_q:
                    norm_q.pop(0)[0]()
                drain(S)
                for f in range(8):
                    yb = ybanks[ycnt % 2]
                    ycnt += 1
                    for hp in range(8):
                        P.mm(ps[yb][:, :], WO[:, hp, f * 128:(f + 1) * 128], OT[:, hp, :], hp == 0, hp == 7,
                             reads=["WO%d" % hp, "OT%d" % hp], writes=bk(S, yb))
                    epi_a(S, f, ps[yb][:, :], bk(S, yb), XF[:, f, :], xf_keys[f], 5, 7, lag=1)
                nxt = (lambda bi=bi: load_xf(bi + 1)) if bi + 1 < nb else None
                epi_b(S, XF, xf_keys, 5, 7, 0, 1, layer,
                      fm(XS[dst])[:, :, qs:qs + 512], fm(XSb[dst])[:, :, qs:qs + 512], after=nxt,
                      use_pool=False, defer_head=True)
            drain(S)

        def ffn_pass(layer, src, tiles, dst, final_out=False):
            P.barrier()
            A.reset()
            RING = [A.alloc([128, 4, 8, 128], BF16) for _ in range(6)]
            XB = A.alloc([128, 8, 1024], BF16)
            XF = A.alloc([128, 8, 1024], F32)
            AT = A.alloc([128, 22, 1024], BF16)
            SGT = [A.alloc([128, 512], F32) for _ in range(2)]
            S = ln_alloc(2)
            sg = 0
            PFW = 4
            wl = []
            for ti in range(len(tiles)):
                wl += [("i", gq) for gq in range(11)] + [("o", f) for f in range(8)]
            issued = [0]

            def rgo_view(slot):
                return RING[slot].rearrange("p a b c -> p (a b c)")[:, 0:22 * 128].rearrange(
                    "p (c n) -> p c n", c=22)

            def issue_w(upto):
                while issued[0] <= upto and issued[0] < len(wl):
                    i = issued[0]
                    slot = i % 6
                    kind, j = wl[i]
                    if kind == "i":
                        P.dma("pool", RING[slot].rearrange("p a b c -> p (a b c)"), wfi[layer, j],
                              writes=["RING%d" % slot])
                    else:
                        P.dma("pool", rgo_view(slot), wfo[layer, j].rearrange("p (c n) -> p c n", c=22),
                              writes=["RING%d" % slot])
                    issued[0] += 1

            xb_keys = ["XBh0", "XBh1"]
            xf_keys = [["XF_%d_%d" % (nt, f) for f in range(8)] for nt in range(2)]

            def load_xb(ti):
                t0, T = tiles[ti]
                for hf in range(2):
                    P.dma("sp", XB[:, 4 * hf:4 * hf + 4, 0:T], fm(XSb[src])[:, 4 * hf:4 * hf + 4, t0:t0 + T],
                          writes=["XBh%d" % hf])

            def load_xf(ti):
                t0, T = tiles[ti]
                for nt in range(T // 512):
                    for hf in range(2):
                        P.dma("sp", XF[:, 4 * hf:4 * hf + 4, nt * 512:(nt + 1) * 512],
                              fm(XS[src])[:, 4 * hf:4 * hf + 4, t0 + nt * 512:t0 + (nt + 1) * 512],
                              writes=xf_keys[nt][4 * hf:4 * hf + 4], slot="XFld%d_%d" % (nt, hf))

            load_xb(0)
            issue_w(PFW)
            load_xf(0)
            wpos = 0
            for ti, (t0, T) in enumerate(tiles):
                nnt = T // 512
                for gq in range(11):
                    if gq >= 1:
                        drain(S, 2)
                    issue_w(wpos + PFW)
                    slot = wpos % 6
                    wpos += 1
                    rg = RING[slot]
                    rk = "RING%d" % slot
                    for j in range(2):
                        c = 2 * gq + j
                        gb = [0, 1] if c % 2 == 0 else [4, 5]
                        hb = [2, 3] if c % 2 == 0 else [6, 7]
                        for (banks, wsel) in ((gb, 2 * j), (hb, 2 * j + 1)):
                            for k in range(8):
                                for nt in range(nnt):
                                    P.mm(ps[banks[nt]][:, :], rg[:, wsel, k, :], XB[:, k, nt * 512:(nt + 1) * 512],
                                         k == 0, k == 7, reads=[rk, xb_keys[k // 4]], writes=[PSK[banks[nt]]])
                        for nt in range(nnt):
                            sgt = SGT[sg % 2]
                            sgk = "SGT%d" % (sg % 2)
                            sg += 1
                            gps, hps = ps[gb[nt]], ps[hb[nt]]
                            P.op("act", lambda e, sgt=sgt, gps=gps: e.activation(out=sgt, in_=gps[:, :], func=AF.Silu),
                                 reads=[PSK[gb[nt]]], writes=[sgk])
                            atv = AT[:, c, nt * 512:(nt + 1) * 512]
                            P.op("dve", lambda e, atv=atv, hps=hps, sgt=sgt: e.tensor_tensor(
                                out=atv, in0=hps[:, :], in1=sgt, op=ALU.mult),
                                 reads=[PSK[hb[nt]], sgk], writes=["AT%d_%d" % (c, nt)])
                drain(S)
                if ti + 1 < len(tiles):
                    load_xb(ti + 1)
                for f in range(8):
                    issue_w(wpos + PFW)
                    slot = wpos % 6
                    wpos += 1
                    rgo = rgo_view(slot)
                    rk = "RING%d" % slot
                    yb = [0, 1] if f % 2 == 0 else [2, 3]
                    for c in range(22):
                        for nt in range(nnt):
                            P.mm(ps[yb[nt]][:, :], rgo[:, c, :], AT[:, c, nt * 512:(nt + 1) * 512],
                                 c == 0, c == 21, reads=[rk, "AT%d_%d" % (c, nt)], writes=[PSK[yb[nt]]])
                    for nt in range(nnt):
                        epi_a(S, f, ps[yb[nt]][:, :], PSK[yb[nt]], XF[:, f, nt * 512:(nt + 1) * 512],
                              xf_keys[nt][f], 4 + nt, 6 + nt, use_pool=False, lag=2)
                for nt in range(nnt):
                    ta = t0 + nt * 512
                    if final_out:
                        d32 = fm(yT)[:, :, ta - 512:ta]
                        d16 = None
                    else:
                        d32 = fm(XS[dst])[:, :, ta:ta + 512]
                        d16 = fm(XSb[dst])[:, :, ta:ta + 512]
                    nxt = None
                    if nt == nnt - 1 and ti + 1 < len(tiles):
                        nxt = lambda ti=ti: load_xf(ti + 1)
                    epi_b(S, XF[:, :, nt * 512:(nt + 1) * 512], xf_keys[nt], 4 + nt, 6 + nt, 2, 3, layer,
                          d32, d16, final=final_out, use_pool=False, nt=nt, after=nxt, defer_head=True)
            drain(S)

        def sg_pass(li, layer, src, ta, tb, dst):
            P.barrier()
            A.reset()
            WR = A.alloc([128, 8, 3072], BF16)
            XBs = [A.alloc([128, 8, 512], BF16) for _ in range(2)]
            XF = A.alloc([128, 8, 512], F32)
            U = A.alloc([128, 8, 512], F32)
            GT = A.alloc([128, 8, 512], BF16)
            VG = [A.alloc([128, 1024], F32) for _ in range(4)]
            VNB = [A.alloc([128, 1024], BF16) for _ in range(4)]
            BS2 = A.alloc([128, 8, 128], F32)
            WS = A.alloc([128, 16, 128], BF16)
            BS = A.alloc([128, 8, 128], F32)
            MIX = [A.alloc([128, 8, 128], F32) for _ in range(2)]
            ST = A.alloc([128, 4, 12], F32)
            MV = A.alloc([128, 4, 2], F32)
            RS = A.alloc([128, 4], F32)
            NBV = A.alloc([128, 4], F32)
            S = ln_alloc(1)
            tlist = list(range(ta, tb, 512))
            ntl = len(tlist)
            xf_keys = ["XF_%d" % f for f in range(8)]

            def load_xb(ti):
                t0 = tlist[ti]
                P.dma("sp", XBs[ti % 2], fm(XSb[src])[:, :, t0:t0 + 512], writes=["XB%d" % (ti % 2)])

            def load_xf(ti):
                t0 = tlist[ti]
                for hf in range(2):
                    P.dma("sp", XF[:, 4 * hf:4 * hf + 4, :], fm(XS[src])[:, 4 * hf:4 * hf + 4, t0:t0 + 512],
                          writes=xf_keys[4 * hf:4 * hf + 4], slot="XFld%d" % hf)

            load_xb(0)
            wi3 = wsgi[li].rearrange("p (k n) -> p k n", k=8)
            wo3 = wsgo[li].rearrange("p (k n) -> p k n", k=8)
            for j in (2, 3, 0, 1):
                P.dma("pool", WR[:, :, j * 512:(j + 1) * 512], wi3[:, :, j * 512:(j + 1) * 512],
                      writes=["WR%d" % j])
            P.dma("pool", WS.rearrange("p a b -> p (a b)"), wsT[li], writes=["WS"])
            for j in (4, 5):
                P.dma("pool", WR[:, :, j * 512:(j + 1) * 512], wo3[:, :, (j - 4) * 512:(j - 3) * 512],
                      writes=["WR%d" % j])
            P.dma("sp", BS.rearrange("p a b -> p (a b)"), bsb[li], writes=["BS"])
            load_xf(0)
            if ntl > 1:
                load_xb(1)
            bank = [0]
            mcnt = [0]
            ybanks = [5, 2]
            ycnt = [0]

            def v_chunks(ti, tcs):
                XB = XBs[ti % 2]
                xk = "XB%d" % (ti % 2)
                for tc in tcs:
                    vg = VG[tc]
                    for hf in range(2):
                        b = bank[0] % 2
                        bank[0] += 1
                        for k in range(8):
                            P.mm(ps[b][:, :], XB[:, k, tc * 128:(tc + 1) * 128],
                                 WR[:, k, 1024 + hf * 512:1024 + (hf + 1) * 512], k == 0, k == 7,
                                 reads=["WR%d" % (2 + hf), xk], writes=[PSK[b]])
                        vgh = vg[:, hf * 512:(hf + 1) * 512]
                        P.op("act", lambda e, vgh=vgh, b=b: e.activation(out=vgh, in_=ps[b][:, :], func=AF.Gelu),
                             reads=[PSK[b]], writes=["VG%d_%d" % (tc, hf)])
                        sto = ST[:, tc, hf * 6:(hf + 1) * 6]
                        P.op("dve", lambda e, sto=sto, vgh=vgh: e.bn_stats(sto, vgh),
                             reads=["VG%d_%d" % (tc, hf)], writes=["ST%d_%d" % (tc, hf)])
                    P.op("dve", lambda e, tc=tc: e.bn_aggr(MV[:, tc, :], ST[:, tc, :]),
                         reads=["ST%d_0" % tc, "ST%d_1" % tc], writes=["MV%d" % tc])

            mvk = ["MV%d" % tc for tc in range(4)]

            def rstd():
                P.op("dve", lambda e: e.tensor_scalar_add(out=RS[:, :], in0=MV[:, :, 1], scalar1=EPS),
                     reads=mvk, writes=["RS"])
                P.op("act", lambda e: e.activation(out=RS[:, :], in_=RS[:, :], func=AF.Ln),
                     reads=["RS"], writes=["RS"])
                P.op("act", lambda e: e.activation(out=RS[:, :], in_=RS[:, :], func=AF.Exp, scale=-0.5),
                     reads=["RS"], writes=["RS"])
                P.op("dve", lambda e: e.scalar_tensor_tensor(out=NBV[:, :], in0=MV[:, :, 0], scalar=-1.0,
                                                             in1=RS[:, :], op0=ALU.mult, op1=ALU.mult),
                     reads=mvk + ["RS"], writes=["NBV"])

            v_chunks(0, range(4))
            rstd()
            rbanks = [0, 1, 2, 5]
            wsf = WS.rearrange("p a b -> p (a b)")
            for j in range(4):
                P.mm(ps[rbanks[j]][:, :], ONES128[:, :], wsf[:, j * 512:(j + 1) * 512], True, True,
                     reads=["ones128", "WS"], writes=[PSK[rbanks[j]]])
            for fc in range(8):
                for hh in range(2):
                    g = 2 * fc + hh
                    rb = rbanks[g // 4]
                    col = (g % 4) * 128
                    bcol = (li * 2 + 1) * 8 + fc
                    sl = slice(hh * 64, (hh + 1) * 64)
                    P.op("dve", lambda e, sl=sl, rb=rb, col=col, bcol=bcol, fc=fc: e.scalar_tensor_tensor(
                        out=BS2[sl, fc, :], in0=ps[rb][sl, col:col + 128], scalar=SGFM[sl, bcol:bcol + 1],
                        in1=BS[sl, fc, :], op0=ALU.mult, op1=ALU.add),
                         reads=[PSK[rb], "sgfm", "BS"], writes=["BS2_%d_%d" % (fc, hh)])
            bs2_keys = ["BS2_%d_%d" % (fc, hh) for fc in range(8) for hh in range(2)]
            for ti, t0 in enumerate(tlist):
                XB = XBs[ti % 2]
                xk = "XB%d" % (ti % 2)
                for fo in range(8):
                    b = bank[0] % 2
                    bank[0] += 1
                    for k in range(8):
                        P.mm(ps[b][:, :], WR[:, k, fo * 128:(fo + 1) * 128], XB[:, k, :], k == 0, k == 7,
                             reads=["WR%d" % (fo // 4), xk], writes=[PSK[b]])
                    uv = U[:, fo, :]
                    P.op("act", lambda e, uv=uv, b=b: e.activation(out=uv, in_=ps[b][:, :], func=AF.Gelu),
                         reads=[PSK[b]], writes=["U%d" % fo])
                    drain(S, 2 if fo == 0 else 1)
                drain(S)
                for tc in range(4):
                    vg, vnb = VG[tc], VNB[tc]
                    P.op("dve", lambda e, vg=vg, vnb=vnb, tc=tc: e.tensor_scalar(
                        out=vnb, in0=vg, scalar1=RS[:, tc:tc + 1], scalar2=NBV[:, tc:tc + 1],
                        op0=ALU.mult, op1=ALU.add),
                         reads=["VG%d_0" % tc, "VG%d_1" % tc, "RS", "NBV"], writes=["VNB%d" % tc])
                if ti + 1 < ntl:
                    if ti + 2 < ntl:
                        load_xb(ti + 2) if False else None
                    v_chunks(ti + 1, (0, 1))
                for tc in range(4):
                    vnb = VNB[tc]
                    ci = mcnt[0] % 2
                    mcnt[0] += 1
                    for fc in range(8):
                        mb = 3 + fc // 4
                        col = (fc % 4) * 128
                        for hh in range(2):
                            g = 2 * fc + hh
                            P.mm(ps[mb][hh * 64:(hh + 1) * 64, col:col + 128], vnb[:, g * 64:(g + 1) * 64],
                                 WS[:, g, :], True, True, reads=["VNB%d" % tc, "WS"],
                                 writes=["M%d_%d" % (mb, hh)], tp=(0, 64 * hh))
                    mix = MIX[ci]
                    for fc in range(8):
                        mb = 3 + fc // 4
                        col = (fc % 4) * 128
                        gcol = (li * 2) * 8 + fc
                        P.op("dve", lambda e, mix=mix, mb=mb, col=col, gcol=gcol, fc=fc: e.scalar_tensor_tensor(
                            out=mix[:, fc, :], in0=ps[mb][:, col:col + 128], scalar=SGFM[:, gcol:gcol + 1],
                            in1=BS2[:, fc, :], op0=ALU.mult, op1=ALU.add),
                             reads=["M%d_0" % mb, "M%d_1" % mb, "sgfm", "BS2_%d_0" % fc, "BS2_%d_1" % fc],
                             writes=["MIX%d_%d" % (ci, fc)])
                    gv = GT[:, :, tc * 128:(tc + 1) * 128]
                    uvv = U[:, :, tc * 128:(tc + 1) * 128]
                    P.op("pool", lambda e, gv=gv, mix=mix, uvv=uvv: e.tensor_tensor(out=gv, in0=mix, in1=uvv,
                                                                                 op=ALU.mult),
                         reads=["MIX%d_%d" % (ci, fc) for fc in range(8)] + ["U%d" % fo for fo in range(8)],
                         writes=["GT%d" % tc])
                if ti + 1 < ntl:
                    v_chunks(ti + 1, (2, 3))
                gt_keys = ["GT%d" % tc for tc in range(4)]
                if ti + 1 < ntl:
                    rstd()
                for f in range(8):
                    yb = ybanks[ycnt[0] % 2]
                    ycnt[0] += 1
                    for k in range(8):
                        P.mm(ps[yb][:, :], WR[:, k, 2048 + f * 128:2048 + (f + 1) * 128], GT[:, k, :], k == 0, k == 7,
                             reads=["WR%d" % (4 + f // 4)] + gt_keys, writes=[PSK[yb]])
                    epi_a(S, f, ps[yb][:, :], PSK[yb], XF[:, f, :], xf_keys[f], 6, 7, use_pool=False, lag=2)
                nxt = (lambda ti=ti: load_xf(ti + 1)) if ti + 1 < ntl else None
                epi_b(S, XF, xf_keys, 6, 7, 0, 1, layer,
                      fm(XS[dst])[:, :, t0:t0 + 512], fm(XSb[dst])[:, :, t0:t0 + 512], use_pool=False,
                      after=nxt)
                if ti + 1 < ntl:
                    if ti + 2 < ntl:
                        load_xb(ti + 2)
            drain(S)

        big = [(256 + 1024 * i, 1024) for i in range(4)] + [(4352, 512)]
        small = [(512 + 1024 * i, 1024) for i in range(4)]
        qkv_pass(0, None, 0, NTOK)
        att_pass(0, 0, x0T, [256 + 512 * b for b in range(9)], {0: "s0", 8: "e0"}, 0)
        ffn_pass(0, 0, big, 1)
        sg_pass(0, 1, 1, 256, 4864, 0)
        ffn_pass(1, 0, big, 1)
        qkv_pass(1, XSb[1], 256, 4864)
        att_pass(1, 2, XS[1], [512 + 512 * b for b in range(8)], {0: "s2", 7: "e2"}, 0)
        ffn_pass(2, 0, small, 1)
        sg_pass(1, 3, 1, 512, 4608, 0)
        ffn_pass(3, 0, small, None, final_out=True)
        P.emit()
    return nc


def _core_geom(i):
    if i < 4:
        return 0, None, 64 * i, 256
    j = i - 4
    return 1, j // 2, 64 * (j % 2), 128


def _shared_inputs(na_w_in, na_rpb, na_w_out, sg_w_in, sg_ln_g, sg_ln_b, sg_w_s, sg_b_s, sg_w_out,
                   ln_mix_g, ln_mix_b, ffn_w_in, ffn_w_out, ln_ffn_g, ln_ffn_b):
    f32 = np.float32
    c = np.ascontiguousarray

    def kmaj(w, n):
        L = w.shape[0]
        return c(w.reshape(L, 8, 128, n).transpose(0, 2, 1, 3).reshape(L, 128, 8 * n)).astype(f32)

    sh = {}
    sh["wqkv"] = kmaj(na_w_in, 3072)
    sh["wona"] = kmaj(na_w_out, 1024)
    sh["wsgi"] = kmaj(sg_w_in, 2048)
    sh["wsgo"] = kmaj(sg_w_out, 1024)
    sh["wsT"] = c(sg_w_s.transpose(0, 3, 1, 2).reshape(2, 128, 16 * 128)).astype(f32)
    bs = sg_b_s.reshape(2, 8, 2, 128)
    bs = np.repeat(bs[:, :, :, None, :], 64, axis=3)
    sh["bsb"] = c(bs.reshape(2, 8, 128, 128).transpose(0, 2, 1, 3).reshape(2, 128, 8 * 128)).astype(f32)
    gb = np.stack([sg_ln_g, sg_ln_b], axis=1)
    sh["sgfm"] = c(gb.reshape(2, 2, 8, 128).transpose(3, 0, 1, 2).reshape(128, 32)).astype(f32)
    w = ffn_w_in.reshape(4, 8, 128, 2, 11, 2, 128)
    sh["wfi"] = c(w.transpose(0, 4, 2, 5, 3, 1, 6).reshape(4, 11, 128, 4 * 8 * 128)).astype(f32)
    w = ffn_w_out.reshape(4, 22, 128, 8, 128)
    sh["wfo"] = c(w.transpose(0, 3, 2, 1, 4).reshape(4, 8, 128, 22 * 128)).astype(f32)
    tabs = np.stack([ln_mix_g, ln_mix_b, ln_ffn_g, ln_ffn_b], axis=0)
    sh["lnp"] = c(tabs.reshape(4, 4, 8, 128).transpose(3, 0, 1, 2).reshape(128, 128)).astype(f32)
    kc = np.arange(64)[:, None]
    qc = np.arange(64)[None, :]
    cs = np.clip(qc - 8, 0, 48)
    colvalid = (kc >= cs) & (kc < cs + 16)
    dcol = np.clip(kc - qc + 15, 0, 30)
    trt = np.zeros((2, 16, 128, 22, 64), f32)
    for n in range(22):
        delta = 10 - n
        for half in range(2):
            dr = delta + half
            if abs(dr) <= 7:
                vals = na_rpb[:, :, dr + 7, :][:, :, dcol]
                vals = np.where(colvalid[None, None], vals, f32(NEG))
            else:
                vals = np.broadcast_to(np.where(colvalid, f32(0.0), f32(NEG))[None, None], (2, 16, 64, 64))
            trt[:, :, half * 64:(half + 1) * 64, n, :] = vals
    sh["trt"] = c(trt.reshape(2, 16, 128, 22 * 64))
    qd = np.zeros((8, 8, 64), f32)
    for m in range(8):
        qd[m, m, :] = 1.0
    sh["qdl"] = qd.reshape(8, 512)
    return sh


def _core_inputs(i, x_prompt, x_sample):
    f32 = np.float32
    grp, b, a, rows = _core_geom(i)
    x = x_prompt[0] if grp == 0 else x_sample[b]
    xg = x.reshape(rows, 64, D)
    buf = np.zeros((80, 64, D), f32)
    lo = max(0, a - 8)
    hi = min(rows, a + 72)
    buf[lo - (a - 8):hi - (a - 8)] = xg[lo:hi]
    x0T = np.ascontiguousarray(buf.reshape(NTOK, D).T)
    msk = np.zeros((2, 9, 8, 16, 64), f32)
    for li in range(2):
        nb = 9 if li == 0 else 8
        for bi in range(nb):
            s = (-4 + 8 * bi) if li == 0 else 8 * bi
            for m in range(8):
                ql = s + m
                qr = a + ql
                for n in range(16):
                    kl = s - 4 + n
                    kr = a + kl
                    if 0 <= qr < rows:
                        rs = min(max(qr - 4, 0), rows - 8)
                        valid = rs <= kr < rs + 8
                    else:
                        valid = -4 <= kl - ql <= 3
                    if not valid:
                        msk[li, bi, m, n, :] = NEG
                    else:
                        kind = {(0, 0): "s0", (0, 8): "e0", (1, 0): "s2", (1, 7): "e2"}.get((li, bi))
                        i0, i1 = band_ranges(kind)[n // 2]
                        assert i0 <= m <= i1, ("window outside computed band", i, li, bi, m, n)
    return {"x0T": x0T, "msk": msk.reshape(2, 9, 8, 1024)}


_NC_CACHE = {}


def kernel(x_prompt, x_sample, na_w_in, na_rpb, na_w_out, sg_w_in, sg_ln_g, sg_ln_b,
           sg_w_s, sg_b_s, sg_w_out, ln_mix_g, ln_mix_b, ffn_w_in, ffn_w_out, ln_ffn_g, ln_ffn_b):
    args = [np.asarray(v, dtype=np.float32) for v in (
        x_prompt, x_sample, na_w_in, na_rpb, na_w_out, sg_w_in, sg_ln_g, sg_ln_b,
        sg_w_s, sg_b_s, sg_w_out, ln_mix_g, ln_mix_b, ffn_w_in, ffn_w_out, ln_ffn_g, ln_ffn_b)]
    x_prompt, x_sample = args[0], args[1]
    shared = _shared_inputs(*args[2:])
    in_maps = []
    for i in range(8):
        m = dict(shared)
        m.update(_core_inputs(i, x_prompt, x_sample))
        in_maps.append(m)
    if "nc" not in _NC_CACHE:
        _NC_CACHE["nc"] = build_program()
    nc = _NC_CACHE["nc"]
    res = run_bass_kernel_spmd(nc, in_maps, core_ids=list(range(8)))
    y_prompt = np.zeros((1, 16384, D), np.float32)
    y_sample = np.zeros((2, 8192, D), np.float32)
    for i in range(8):
        grp, b, a, rows = _core_geom(i)
        yt = np.asarray(res.results[i]["yT"], dtype=np.float32)
        blk = yt.T
        if grp == 0:
            y_prompt[0, a * 64:a * 64 + 4096] = blk
        else:
            y_sample[b, a * 64:a * 64 + 4096] = blk
    return (y_prompt, y_sample)
```

```python
import numpy as np
from contextlib import ExitStack
import concourse.bass as bass
import concourse.mybir as mybir
from concourse.bass_utils import run_bass_kernel_spmd

F32 = mybir.dt.float32
BF16 = mybir.dt.bfloat16
AF = mybir.ActivationFunctionType
ALU = mybir.AluOpType

D = 1024
NTOK = 5120
ALPHA = float(8.0 ** 0.25)
EPS = 1e-5
NEG = -30000.0
ENGS = ["pe", "act", "dve", "pool", "sp"]
ARENA_BYTES = 189 * 1024


class Op:
    __slots__ = ("eng", "fn", "dma", "slot", "deps", "needs_inc", "sem", "val", "grp")

    def __init__(self, eng, fn, dma, slot):
        self.eng = eng
        self.fn = fn
        self.dma = dma
        self.slot = slot
        self.grp = None
        self.deps = []
        self.needs_inc = False
        self.sem = None
        self.val = None


class Prog:
    def __init__(self, nc):
        self.nc = nc
        self.ops = []
        self.last_writer = {}
        self.readers = {}
        self.final_dmas = []
        self.last_eng = {}
        self.last_slot = {}
        self.bar = None
        self.bar_done = set()

    def barrier(self):
        deps = list(self.last_eng.values()) + list(self.last_slot.values())
        self.bar = deps
        self.bar_done = set()
        self.last_writer = {}
        self.readers = {}

    def op(self, eng, fn, reads=(), writes=(), dma=False, slot=None, final=False):
        o = Op(eng, fn, dma, slot)
        deps = set()
        for r in reads:
            w = self.last_writer.get(r)
            if w is not None:
                deps.add(w)
        for r in writes:
            w = self.last_writer.get(r)
            if w is not None:
                deps.add(w)
            for rd in self.readers.get(r, ()):
                deps.add(rd)
        if self.bar is not None and eng not in self.bar_done:
            self.bar_done.add(eng)
            for d in self.bar:
                deps.add(d)
        for d in deps:
            if (not o.dma) and (not d.dma) and o.eng == "pe" and d.eng == "pe":
                continue
            o.deps.append(d)
        for r in writes:
            self.last_writer[r] = o
            self.readers[r] = []
        for r in reads:
            self.readers.setdefault(r, []).append(o)
        self.ops.append(o)
        if dma:
            self.last_slot[slot] = o
        else:
            self.last_eng[eng] = o
        if final:
            self.final_dmas.append(o)
        return o

    def dma(self, q, out, in_, reads=(), writes=(), slot=None, final=False, grp=None):
        if grp is not None:
            slot = grp
        elif slot is None:
            slot = writes[0] if writes else reads[0]
        slot = q + ":" + slot
        o = self.op(q, lambda e, out=out, in_=in_: e.dma_start(out=out, in_=in_),
                    reads=reads, writes=writes, dma=True, slot=slot, final=final)
        o.grp = grp
        return o

    def mm(self, out, lhsT, rhs, start, stop, reads, writes, tp=None):
        if tp is None:
            fn = lambda e: e.matmul(out, lhsT, rhs, start=start, stop=stop)
        else:
            fn = lambda e: e.matmul(out, lhsT, rhs, start=start, stop=stop, tile_position=tp)
        return self.op("pe", fn, reads=reads, writes=writes)

    def emit(self):
        nc = self.nc
        for o in self.ops:
            for d in o.deps:
                d.needs_inc = True
        for o in self.final_dmas:
            o.needs_inc = True
        slots = []
        seen = set()
        for o in self.ops:
            if o.dma and o.needs_inc and o.slot not in seen:
                seen.add(o.slot)
                slots.append(o.slot)
        self.n_slots = len(slots)
        with ExitStack() as es:
            esem = {e: es.enter_context(nc.semaphore("e_" + e)) for e in ENGS}
            ssem = {s: es.enter_context(nc.semaphore("d%d" % i)) for i, s in enumerate(slots)}
            ecnt = {e: 0 for e in ENGS}
            scnt = {s: 0 for s in slots}
            for o in self.ops:
                if o.dma and o.grp is not None and o.slot in ssem:
                    o.needs_inc = True
                if not o.needs_inc:
                    continue
                if o.dma:
                    scnt[o.slot] += 16
                    o.sem = ssem[o.slot]
                    o.val = scnt[o.slot]
                else:
                    ecnt[o.eng] += 1
                    o.sem = esem[o.eng]
                    o.val = ecnt[o.eng]
            run = []
            for o in self.ops + [None]:
                if o is not None and o.dma and o.grp is not None and o.needs_inc:
                    if run and run[-1].slot != o.slot:
                        for r in run:
                            r.val = run[-1].val
                        run = []
                    run.append(o)
                elif o is None or (o.dma and o.needs_inc):
                    for r in run:
                        r.val = run[-1].val
                    run = []
            block = es.enter_context(nc.Block())
            per = {e: [o for o in self.ops if o.eng == e] for e in ENGS}
            engobj = {"pe": nc.tensor, "act": nc.scalar, "dve": nc.vector, "pool": nc.gpsimd,
                      "sp": nc.sync}
            finals = self.final_dmas

            def make(e):
                def body(h):
                    waited = {}
                    for o in per[e]:
                        need = {}
                        for d in o.deps:
                            k = id(d.sem)
                            if k not in need or need[k][1] < d.val:
                                need[k] = (d.sem, d.val)
                        for k, (sem, val) in need.items():
                            if waited.get(k, 0) >= val:
                                continue
                            h.wait_ge(sem, val)
                            waited[k] = val
                        ins = o.fn(engobj[e])
                        if o.needs_inc:
                            ins.then_inc(o.sem, 16 if o.dma else 1)
                    if e == "sp":
                        for o in finals:
                            h.wait_ge(o.sem, o.val)
                return body

            block.tensor(make("pe"))
            block.scalar(make("act"))
            block.vector(make("dve"))
            block.gpsimd(make("pool"))
            block.sync(make("sp"))


def band_ranges(kind):
    r = [(max(0, 2 * p - 7), min(7, 2 * p + 1)) for p in range(8)]
    ext = {
        None: {},
        "s0": {6: (4, 7), 7: (4, 7)},
        "e0": {0: (0, 3)},
        "s2": {4: (0, 7), 5: (0, 7)},
        "e2": {2: (0, 7)},
    }[kind]
    for p, v in ext.items():
        r[p] = (min(r[p][0], v[0]), max(r[p][1], v[1]))
    return r


class Arena:
    def __init__(self, t):
        self.t = t
        self.off = 0

    def reset(self):
        self.off = 0

    def alloc(self, shape, dt):
        esz = 4 if dt == F32 else 2
        n = 1
        for s in shape[1:]:
            n *= s
        nbytes = (n * esz + 63) // 64 * 64
        assert self.off + nbytes <= ARENA_BYTES, ("arena overflow", self.off + nbytes)
        v = self.t[:, self.off // 2:(self.off + n * esz) // 2]
        self.off += nbytes
        if dt == F32:
            v = v.bitcast(F32)
        if len(shape) == 3:
            v = v.rearrange("p (a b) -> p a b", a=shape[1])
        elif len(shape) == 4:
            v = v.rearrange("p (a b c) -> p a b c", a=shape[1], b=shape[2])
        return v


def build_program():
    nc = bass.Bass("TRN2", target_bir_lowering=False)

    def din(name, shape):
        return nc.dram_tensor(name, shape, F32, kind="ExternalInput").ap()

    def dscr(name, shape, dt):
        return nc.dram_tensor(name, shape, dt, kind="Internal").ap()

    x0T = din("x0T", [D, NTOK])
    wqkv = din("wqkv", [2, 128, 8 * 3072])
    wona = din("wona", [2, 128, 8 * 1024])
    wsgi = din("wsgi", [2, 128, 8 * 2048])
    wsgo = din("wsgo", [2, 128, 8 * 1024])
    wsT = din("wsT", [2, 128, 16 * 128])
    bsb = din("bsb", [2, 128, 8 * 128])
    sgfm_d = din("sgfm", [128, 32])
    wfi = din("wfi", [4, 11, 128, 4 * 8 * 128])
    wfo = din("wfo", [4, 8, 128, 22 * 128])
    lnp_d = din("lnp", [128, 4 * 4 * 8])
    trt = din("trt", [2, 16, 128, 22 * 64])
    msk = din("msk", [2, 9, 8, 1024])
    qdl = din("qdl", [8, 512])
    yT = nc.dram_tensor("yT", [D, 4096], F32, kind="ExternalOutput").ap()

    XS = [dscr("xs0", [D, NTOK], F32), dscr("xs1", [D, NTOK], F32)]
    XSb = [dscr("xsb0", [D, NTOK], BF16), dscr("xsb1", [D, NTOK], BF16)]
    QT = dscr("qt", [D, NTOK], BF16)
    KT = dscr("kt", [D, NTOK], BF16)
    VV = dscr("vv", [NTOK, D], BF16)
    ETS = dscr("ets", [2, 16, 128, 22 * 64], F32)

    def fm(ap):
        return ap.rearrange("(f p) t -> p f t", p=128)

    with ExitStack() as es:
        arena_t = es.enter_context(nc.sbuf_tensor("arena", [128, ARENA_BYTES // 2], BF16))
        ONESM = es.enter_context(nc.sbuf_tensor("onesm", [128, 128], BF16))
        ONES1 = es.enter_context(nc.sbuf_tensor("ones1", [128, 64], BF16))
        LNP = es.enter_context(nc.sbuf_tensor("lnp_sb", [128, 128], F32))
        SGFM = es.enter_context(nc.sbuf_tensor("sgfm_sb", [128, 32], F32))
        ONES128 = es.enter_context(nc.sbuf_tensor("ones128", [128, 128], BF16))
        ps = [es.enter_context(nc.psum_tensor("ps%d" % i, [128, 512], F32)) for i in range(8)]
        PSK = ["ps%d" % i for i in range(8)]
        A = Arena(arena_t)
        P = Prog(nc)

        P.op("pool", lambda e: e.memset(ONESM[:], 1.0 / 1024.0), writes=["onesm"])
        P.op("pool", lambda e: e.memset(ONES1[:], 1.0), writes=["ones1"])
        P.dma("sp", LNP[:], lnp_d, writes=["lnp"])
        P.dma("sp", SGFM[:], sgfm_d, writes=["sgfm"])
        P.op("pool", lambda e: e.memset(ONES128[:], 1.0), writes=["ones128"])

        def lnp_col(tab, l, f):
            c = (tab * 4 + l) * 8 + f
            return LNP[:, c:c + 1]

        class LNState:
            pass

        def ln_alloc(n_nt):
            S = LNState()
            S.sbt = [A.alloc([128, 512], BF16) for _ in range(4)]
            S.s2t = [A.alloc([128, 512], BF16) for _ in range(4)]
            S.ms = [A.alloc([128, 512], F32) for _ in range(n_nt)]
            S.m2 = A.alloc([128, 512], F32)
            S.var = [A.alloc([128, 512], F32) for _ in range(n_nt)]
            S.a = [A.alloc([128, 512], F32) for _ in range(n_nt)]
            S.b = [A.alloc([128, 512], F32) for _ in range(n_nt)]
            S.t1 = [A.alloc([128, 512], F32) for _ in range(2)]
            S.cnt = 0
            S.tcnt = 0
            S.pending = []
            S.tail = []
            return S

        def drain(S, n=None):
            while S.tail and (n is None or n > 0):
                S.tail.pop(0)[1]()
                if n is not None:
                    n -= 1

        BANK_KEYS = {3: ["O0_0", "O0_1"], 4: ["Z0_0", "Z0_1"], 5: ["O1_0", "O1_1"], 6: ["Z1_0", "Z1_1"]}

        def bk(S, b):
            return [PSK[b]] + (BANK_KEYS.get(b, []) if getattr(S, "oz_banks", False) else [])

        def epi_stats(S, ent):
            f, i, mean_b, ex2_b = ent
            P.mm(ps[mean_b][:, :], ONESM[:, :], S.sbt[i], f == 0, f == 7,
                 reads=["sbt%d" % i, "onesm"], writes=bk(S, mean_b))
            P.mm(ps[ex2_b][:, :], ONESM[:, :], S.s2t[i], f == 0, f == 7,
                 reads=["s2t%d" % i, "onesm"], writes=bk(S, ex2_b))

        def epi_a(S, f, y_ps, y_key, xf_f, xf_key, mean_b, ex2_b, use_pool=True, lag=2):
            i = S.cnt % 4
            S.cnt += 1
            P.op("dve", lambda e: e.scalar_tensor_tensor(out=xf_f, in0=xf_f, scalar=ALPHA, in1=y_ps,
                                                         op0=ALU.mult, op1=ALU.add),
                 reads=(y_key if isinstance(y_key, list) else [y_key]) + [xf_key], writes=[xf_key])
            sbt, s2t = S.sbt[i], S.s2t[i]
            P.op("act", lambda e: e.copy(sbt, xf_f), reads=[xf_key], writes=["sbt%d" % i])
            if use_pool:
                P.op("pool", lambda e: e.tensor_tensor(out=s2t, in0=xf_f, in1=xf_f, op=ALU.mult),
                     reads=[xf_key], writes=["s2t%d" % i])
            else:
                P.op("act", lambda e: e.activation(out=s2t, in_=xf_f, func=AF.Square),
                     reads=[xf_key], writes=["s2t%d" % i])
            S.pending.append((f, i, mean_b, ex2_b))
            while len(S.pending) > lag:
                epi_stats(S, S.pending.pop(0))

        def epi_flush(S):
            while S.pending:
                epi_stats(S, S.pending.pop(0))

        def epi_b(S, xf, xf_keys, mean_b, ex2_b, gtab, btab, l, dst_f32, dst_bf16, final=False,
                  use_pool=True, nt=0, after=None, defer_head=False):
            epi_flush(S)
            ms, sa, sb_, var = S.ms[nt], S.a[nt], S.b[nt], S.var[nt]
            kms, ka_, kb_, kv = "ms%d" % nt, "lna%d" % nt, "lnb%d" % nt, "var%d" % nt
            P.op("act", lambda e: e.copy(ms, ps[mean_b][:, :]), reads=bk(S, mean_b), writes=[kms])
            P.op("pool" if use_pool else "dve",
                 lambda e: e.tensor_tensor(out=S.m2, in0=ms, in1=ms, op=ALU.mult),
                 reads=[kms], writes=["m2"])
            P.op("dve", lambda e: e.tensor_tensor(out=var, in0=ps[ex2_b][:, :], in1=S.m2,
                                                  op=ALU.subtract),
                 reads=bk(S, ex2_b) + ["m2"], writes=[kv])

            def head2():
                P.op("dve", lambda e: e.tensor_scalar_add(out=var, in0=var, scalar1=EPS),
                     reads=[kv], writes=[kv])
                P.op("act", lambda e: e.activation(out=var, in_=var, func=AF.Ln),
                     reads=[kv], writes=[kv])
                P.op("act", lambda e: e.activation(out=sa, in_=var, func=AF.Exp, scale=-0.5),
                     reads=[kv], writes=[ka_])
                P.op("dve", lambda e: e.scalar_tensor_tensor(out=sb_, in0=ms, scalar=-1.0, in1=sa,
                                                             op0=ALU.mult, op1=ALU.mult),
                     reads=[kms, ka_], writes=[kb_])

            if defer_head:
                k = 0
                while k < len(S.tail) and S.tail[k][0] == "h":
                    k += 1
                S.tail.insert(k, ("h", head2))
            else:
                head2()

            tis = {}

            def dve_part(f):
                ti = S.tcnt % 2
                S.tcnt += 1
                tis[f] = ti
                t1 = S.t1[ti]
                tk = "t1_%d" % ti
                xf_f = xf[:, f, :]
                eng = "dve" if (f % 2 == 0 or not use_pool) else "pool"
                P.op(eng, lambda e: e.tensor_tensor(out=t1, in0=xf_f, in1=sa, op=ALU.mult),
                     reads=[xf_keys[f], ka_], writes=[tk])
                P.op(eng, lambda e: e.tensor_tensor(out=t1, in0=t1, in1=sb_, op=ALU.add),
                     reads=[tk, kb_], writes=[tk])

            def act_part(f):
                ti = tis[f]
                t1 = S.t1[ti]
                tk = "t1_%d" % ti
                xf_f = xf[:, f, :]
                g = lnp_col(gtab, l, f)
                b = lnp_col(btab, l, f)
                P.op("act", lambda e: e.activation(out=xf_f, in_=t1, func=AF.Identity, bias=b, scale=g),
                     reads=[tk, "lnp"], writes=[xf_keys[f]])

            def apply_f(f):
                dve_part(f)
                if f >= 1:
                    act_part(f - 1)

            def stores():
                act_part(7)
                for hf in range(2):
                    P.dma("sp", dst_f32[:, 4 * hf:4 * hf + 4, :], xf[:, 4 * hf:4 * hf + 4, :],
                          reads=xf_keys[4 * hf:4 * hf + 4], writes=[], slot=xf_keys[4 * hf] + "_st",
                          final=final)
                if dst_bf16 is not None:
                    for hf in range(2):
                        P.dma("pool", dst_bf16[:, 4 * hf:4 * hf + 4, :], xf[:, 4 * hf:4 * hf + 4, :],
                              reads=xf_keys[4 * hf:4 * hf + 4], writes=[], slot=xf_keys[4 * hf] + "_sb")
                if after is not None:
                    after()

            for f in range(8):
                S.tail.append(("a", lambda f=f: apply_f(f)))
            S.tail.append(("s", stores))

        def qkv_pass(li, src_bf16, ta, tb):
            P.barrier()
            A.reset()
            WR = A.alloc([128, 8, 3072], BF16)
            XBs = [A.alloc([128, 8, 512], BF16) for _ in range(2)]
            QKs = [A.alloc([128, 16, 512], BF16) for _ in range(2)]
            VSs = [A.alloc([128, 4, 1024], BF16) for _ in range(2)]
            TRB = [A.alloc([128, 22 * 64], F32) for _ in range(4)]
            tlist = list(range(ta, tb, 512))
            et_done = [0]

            def et_load(h):
                if h < 16:
                    P.dma("sp", TRB[h % 4], trt[li, h], writes=["TRB%d" % (h % 4)])

            def et_exp(h):
                if h < 16:
                    tb_ = TRB[h % 4]
                    P.op("act", lambda e, tb_=tb_: e.activation(out=tb_, in_=tb_, func=AF.Exp),
                         reads=["TRB%d" % (h % 4)], writes=["TRB%d" % (h % 4)])
                    P.dma("sp", ETS[li, h], tb_, reads=["TRB%d" % (h % 4)], writes=[],
                          slot="TRBst%d" % (h % 4))

            def load_xb(ti):
                t0 = tlist[ti]
                xb = XBs[ti % 2]
                if src_bf16 is None:
                    P.dma("pool", xb, fm(x0T)[:, :, t0:t0 + 512], writes=["XB%d" % (ti % 2)])
                else:
                    P.dma("sp", xb, fm(src_bf16)[:, :, t0:t0 + 512], writes=["XB%d" % (ti % 2)])

            load_xb(0)
            wq3 = wqkv[li].rearrange("p (k n) -> p k n", k=8)
            for j in range(6):
                P.dma("pool", WR[:, :, j * 512:(j + 1) * 512], wq3[:, :, j * 512:(j + 1) * 512],
                      writes=["WR%d" % j])
            bank = 0
            for ti, t0 in enumerate(tlist):
                s = ti % 2
                xb, qk, vs = XBs[s], QKs[s], VSs[s]
                xk = "XB%d" % s
                if ti + 1 < len(tlist):
                    load_xb(ti + 1)
                if ti == 0:
                    et_load(0)
                    et_load(1)
                et_load(2 * ti + 2)
                et_load(2 * ti + 3)
                for fo in range(16):
                    if fo == 6:
                        et_exp(2 * ti)
                    if fo == 12:
                        et_exp(2 * ti + 1)
                    b = bank % 8
                    bank += 1
                    for k in range(8):
                        P.mm(ps[b][:, :], WR[:, k, fo * 128:(fo + 1) * 128], xb[:, k, :], k == 0, k == 7,
                             reads=["WR%d" % (fo // 4), xk], writes=[PSK[b]])
                    dst = qk[:, fo, :]
                    sc = 0.125 if fo < 8 else 1.0
                    if fo % 2 == 0:
                        P.op("act", lambda e, dst=dst, b=b, sc=sc: e.activation(
                            out=dst, in_=ps[b][:, :], func=AF.Copy, scale=sc),
                             reads=[PSK[b]], writes=["QK%d_%d" % (s, fo)])
                    else:
                        P.op("dve", lambda e, dst=dst, b=b, sc=sc: e.tensor_scalar_mul(
                            out=dst, in0=ps[b][:, :], scalar1=sc),
                             reads=[PSK[b]], writes=["QK%d_%d" % (s, fo)])
                P.dma("sp", fm(QT)[:, :, t0:t0 + 512], qk[:, 0:8, :],
                      reads=["QK%d_%d" % (s, fo) for fo in range(8)], slot="QKq%d" % s)
                P.dma("sp", fm(KT)[:, :, t0:t0 + 512], qk[:, 8:16, :],
                      reads=["QK%d_%d" % (s, fo) for fo in range(8, 16)], slot="QKk%d" % s)
                for tg in range(4):
                    for hf in range(2):
                        b = bank % 8
                        bank += 1
                        for k in range(8):
                            P.mm(ps[b][:, :], xb[:, k, tg * 128:(tg + 1) * 128],
                                 WR[:, k, 2048 + hf * 512:2048 + (hf + 1) * 512], k == 0, k == 7,
                                 reads=["WR%d" % (4 + hf), xk], writes=[PSK[b]])
                        dst = vs[:, tg, hf * 512:(hf + 1) * 512]
                        if hf == 0:
                            P.op("act", lambda e, dst=dst, b=b: e.copy(dst, ps[b][:, :]),
                                 reads=[PSK[b]], writes=["VS%d_%d_%d" % (s, tg, hf)])
                        else:
                            P.op("dve", lambda e, dst=dst, b=b: e.tensor_copy(dst, ps[b][:, :]),
                                 reads=[PSK[b]], writes=["VS%d_%d_%d" % (s, tg, hf)])
                P.dma("sp", VV[t0:t0 + 512, :].rearrange("(g p) f -> p g f", p=128), vs,
                      reads=["VS%d_%d_%d" % (s, tg, hf) for tg in range(4) for hf in range(2)],
                      slot="VSst%d" % s)

        def att_pass(li, layer, res_src, qs_list, dense_blocks, dst):
            P.barrier()
            A.reset()
            NR = 10
            SRING = [0, 1, 2, 7]
            NSB = 3
            WO = A.alloc([128, 8, 1024], BF16)
            KA = [A.alloc([128, 2, 1024], BF16) for _ in range(3)]
            QA = [A.alloc([128, 2, 512], BF16) for _ in range(3)]
            TR = [A.alloc([128, 2, 22 * 64], F32) for _ in range(3)]
            VB = [A.alloc([128, 8, 1024], BF16) for _ in range(2)]
            SB = [A.alloc([128, 512], F32) for _ in range(NSB)]
            PT = [A.alloc([128, 512], BF16) for _ in range(NR)]
            RZ = [A.alloc([128, 512], F32) for _ in range(2)]
            OT = A.alloc([128, 8, 512], BF16)
            XF = A.alloc([128, 8, 512], F32)
            MSKB = A.alloc([128, 9, 1024], BF16)
            S = ln_alloc(1)
            nb = len(qs_list)
            nunits = nb * 8
            LA = 7
            PF = 2

            def load_unit(u):
                bi, hp = divmod(u, 8)
                qs = qs_list[bi]
                k0 = qs - 256
                s = u % 3
                P.dma("sp", KA[s][0:64, :, :],
                      KT[hp * 128:(hp + 1) * 128, k0:k0 + 1024].rearrange("(hh d) t -> d hh t", d=64),
                      writes=["KA%d" % s])
                P.dma("sp", QA[s][0:64, :, :],
                      QT[hp * 128:(hp + 1) * 128, qs:qs + 512].rearrange("(hh d) t -> d hh t", d=64),
                      writes=["QA%d" % s])
                P.dma("sp", TR[s], ETS[li, 2 * hp:2 * hp + 2].rearrange("h p n -> p h n"),
                      writes=["TR%d" % s])
                for hh in range(2):
                    P.dma("sp", KA[s][64:72, hh, :], MSKB[64:72, bi, :], reads=["MSKB"],
                          writes=["KM%d_%d" % (s, hh)])

            def load_vb(bi):
                k0 = qs_list[bi] - 256
                for q in range(2):
                    P.dma("sp", VB[bi % 2][:, 4 * q:4 * q + 4, :],
                          VV[k0 + 512 * q:k0 + 512 * (q + 1), :].rearrange("(p t) f -> t p f", t=128),
                          writes=["VB%d_%d" % (bi % 2, q)])

            xf_keys = ["XF_%d" % f for f in range(8)]

            def load_xf(bi):
                qs = qs_list[bi]
                for hf in range(2):
                    P.dma("sp", XF[:, 4 * hf:4 * hf + 4, :], fm(res_src)[:, 4 * hf:4 * hf + 4, qs:qs + 512],
                          writes=xf_keys[4 * hf:4 * hf + 4], slot="XFld%d" % hf)

            for s in range(3):
                P.op("pool", lambda e, s=s: e.memset(KA[s][64:128, :, :], 0.0),
                     writes=["KM%d_0" % s, "KM%d_1" % s])
                P.op("pool", lambda e, s=s: e.memset(QA[s][64:128, :, :], 0.0),
                     writes=["QD%d_0" % s, "QD%d_1" % s])
            P.dma("pool", MSKB[64:72, :, :], msk[li].rearrange("b m k -> m b k"), writes=["MSKB"])
            for s in range(3):
                for hh in range(2):
                    P.dma("pool", QA[s][64:72, hh, :], qdl, writes=["QD%d_%d" % (s, hh)], grp="QD")
            load_vb(0)
            issued = [0]

            def issue_units(upto):
                while issued[0] <= upto and issued[0] < nunits:
                    load_unit(issued[0])
                    issued[0] += 1

            issue_units(PF)
            for k in range(8):
                P.dma("pool", WO[:, k, :], wona[li, :, k * 1024:(k + 1) * 1024], writes=["WO%d" % k], grp="WO")
            ybanks = [3, 4, 6]
            S.oz_banks = True
            ycnt = 0
            gidx = 0
            for bi, qs in enumerate(qs_list):
                vbk = ["VB%d_0" % (bi % 2), "VB%d_1" % (bi % 2)]
                vb = VB[bi % 2]
                if bi == 0:
                    load_xf(0)
                its = []
                rng = band_ranges(dense_blocks.get(bi))
                for hp in range(8):
                    for p in (3, 0, 1, 2, 4, 5, 6, 7):
                        for hh in range(2):
                            i0, i1 = rng[p]
                            its.append((hp, hh, p, i0 * 64, (i1 + 1) * 64, 14 - 2 * p + i0, gidx))
                            gidx += 1
                nit = len(its)

                def emit_front(idx):
                    hp, hh, p, c0, c1, n0, g = its[idx]
                    u = bi * 8 + hp
                    s = u % 3
                    si = SRING[g % 4]
                    ri = g % NR
                    sps = ps[si]
                    ka, qa, tr = KA[s], QA[s], TR[s]
                    P.mm(sps[:, c0:c1], ka[0:128, hh, p * 128:(p + 1) * 128], qa[0:128, hh, c0:c1],
                         True, True,
                         reads=["KA%d" % s, "KM%d_%d" % (s, hh), "QA%d" % s, "QD%d_%d" % (s, hh)],
                         writes=[PSK[si]])
                    sbi = g % NSB
                    sb, pt = SB[sbi], PT[ri]
                    trv = tr[:, hh, n0 * 64:n0 * 64 + (c1 - c0)]
                    P.op("act", lambda e: e.activation(out=sb[:, c0:c1], in_=sps[:, c0:c1], func=AF.Exp),
                         reads=[PSK[si]], writes=["SB%d" % sbi])
                    P.op("dve" if g % 2 == 0 else "pool",
                         lambda e: e.tensor_tensor(out=pt[:, c0:c1], in0=sb[:, c0:c1], in1=trv, op=ALU.mult),
                         reads=["SB%d" % sbi, "TR%d" % s], writes=["PT%d" % ri])

                def emit_back_pair(ia, ib):
                    for kind in ("pv", "z"):
                        for idx in (ia, ib):
                            hp, hh, p, c0, c1, n0, g = its[idx]
                            ri = g % NR
                            pt = PT[ri]
                            st = hp % 2
                            ob, zb = 3 + 2 * st, 4 + 2 * st
                            first = p == 3
                            last = p == 7
                            if kind == "pv":
                                P.mm(ps[ob][hh * 64:(hh + 1) * 64, c0:c1],
                                     vb[:, p, hp * 128 + hh * 64:hp * 128 + (hh + 1) * 64], pt[:, c0:c1],
                                     first, last, reads=[vbk[p // 4], "PT%d" % ri],
                                     writes=["O%d_%d" % (st, hh)], tp=(0, 64 * hh))
                            else:
                                P.mm(ps[zb][hh * 64:(hh + 1) * 64, c0:c1], ONES1[:, :], pt[:, c0:c1],
                                     first, last, reads=["ones1", "PT%d" % ri],
                                     writes=["Z%d_%d" % (st, hh)], tp=(0, 64 * hh))
                    hp, hh, p = its[ib][0:3]
                    if p == 7:
                        def norm_act(hp=hp):
                            st = hp % 2
                            zb = 4 + 2 * st
                            rz = RZ[st]
                            P.op("act", lambda e: e.activation(out=rz, in_=ps[zb][:, :], func=AF.Ln),
                                 reads=["Z%d_0" % st, "Z%d_1" % st], writes=["RZ%d" % st])
                            P.op("act", lambda e: e.activation(out=rz, in_=rz, func=AF.Exp, scale=-1.0),
                                 reads=["RZ%d" % st], writes=["RZ%d" % st])

                        def norm_dve(hp=hp):
                            st = hp % 2
                            ob = 3 + 2 * st
                            rz = RZ[st]
                            otv = OT[:, hp, :]
                            P.op("dve", lambda e: e.tensor_tensor(out=otv, in0=ps[ob][:, :], in1=rz, op=ALU.mult),
                                 reads=["O%d_0" % st, "O%d_1" % st, "RZ%d" % st], writes=["OT%d" % hp])
                        norm_q.append([norm_act, 0, 7])
                        norm_q.append([norm_dve, 0, 10])

                norm_q = []
                for idx in range(nit + LA + 1):
                    for ent in norm_q:
                        ent[1] += 1
                    while norm_q and norm_q[0][1] >= norm_q[0][2]:
                        norm_q.pop(0)[0]()
                    if idx < nit:
                        hp, hh, p = its[idx][0:3]
                        if hh == 0 and p == 0:
                            issue_units(bi * 8 + hp + PF)
                            if hp == 3 and bi + 1 < nb:
                                load_vb(bi + 1)
                        emit_front(idx)
                    j = idx - LA
                    if j >= 1 and j % 2 == 1 and j < nit:
                        emit_back_pair(j - 1, j)
                    if idx % 8 == 4:
                        drain(S, 1)
                while norm_q:
                    norm_q.pop(0)[0]()
                drain(S)
                for f in range(8):
                    yb = ybanks[ycnt % 3]
                    ycnt += 1
                    for hp in range(8):
                        P.mm(ps[yb][:, :], WO[:, hp, f * 128:(f + 1) * 128], OT[:, hp, :], hp == 0, hp == 7,
                             reads=["WO%d" % hp, "OT%d" % hp], writes=bk(S, yb))
                    epi_a(S, f, ps[yb][:, :], bk(S, yb), XF[:, f, :], xf_keys[f], 5, 7, lag=2)
                nxt = (lambda bi=bi: load_xf(bi + 1)) if bi + 1 < nb else None
                epi_b(S, XF, xf_keys, 5, 7, 0, 1, layer,
                      fm(XS[dst])[:, :, qs:qs + 512], fm(XSb[dst])[:, :, qs:qs + 512], after=nxt,
                      use_pool=False, defer_head=True)
            drain(S)

        def ffn_pass(layer, src, tiles, dst, final_out=False):
            P.barrier()
            A.reset()
            RING = [A.alloc([128, 4, 8, 128], BF16) for _ in range(6)]
            XB = A.alloc([128, 8, 1024], BF16)
            XF = A.alloc([128, 8, 1024], F32)
            AT = A.alloc([128, 22, 1024], BF16)
            SGT = [A.alloc([128, 512], F32) for _ in range(2)]
            S = ln_alloc(2)
            sg = 0
            PFW = 4
            wl = []
            for ti in range(len(tiles)):
                wl += [("i", gq) for gq in range(11)] + [("o", f) for f in range(8)]
            issued = [0]

            def rgo_view(slot):
                return RING[slot].rearrange("p a b c -> p (a b c)")[:, 0:22 * 128].rearrange(
                    "p (c n) -> p c n", c=22)

            def issue_w(upto):
                while issued[0] <= upto and issued[0] < len(wl):
                    i = issued[0]
                    slot = i % 6
                    kind, j = wl[i]
                    if kind == "i":
                        P.dma("pool", RING[slot].rearrange("p a b c -> p (a b c)"), wfi[layer, j],
                              writes=["RING%d" % slot])
                    else:
                        P.dma("pool", rgo_view(slot), wfo[layer, j].rearrange("p (c n) -> p c n", c=22),
                              writes=["RING%d" % slot])
                    issued[0] += 1

            xb_keys = ["XBh0", "XBh1"]
            xf_keys = [["XF_%d_%d" % (nt, f) for f in range(8)] for nt in range(2)]

            def load_xb(ti):
                t0, T = tiles[ti]
                for hf in range(2):
                    P.dma("sp", XB[:, 4 * hf:4 * hf + 4, 0:T], fm(XSb[src])[:, 4 * hf:4 * hf + 4, t0:t0 + T],
                          writes=["XBh%d" % hf])

            def load_xf(ti):
                t0, T = tiles[ti]
                for nt in range(T // 512):
                    for hf in range(2):
                        P.dma("sp", XF[:, 4 * hf:4 * hf + 4, nt * 512:(nt + 1) * 512],
                              fm(XS[src])[:, 4 * hf:4 * hf + 4, t0 + nt * 512:t0 + (nt + 1) * 512],
                              writes=xf_keys[nt][4 * hf:4 * hf + 4], slot="XFld%d_%d" % (nt, hf))

            load_xb(0)
            issue_w(PFW)
            load_xf(0)
            wpos = 0
            for ti, (t0, T) in enumerate(tiles):
                nnt = T // 512
                for gq in range(11):
                    if gq >= 1:
                        drain(S, 2)
                    issue_w(wpos + PFW)
                    slot = wpos % 6
                    wpos += 1
                    rg = RING[slot]
                    rk = "RING%d" % slot
                    for j in range(2):
                        c = 2 * gq + j
                        gb = [0, 1] if c % 2 == 0 else [4, 5]
                        hb = [2, 3] if c % 2 == 0 else [6, 7]
                        for (banks, wsel) in ((gb, 2 * j), (hb, 2 * j + 1)):
                            for k in range(8):
                                for nt in range(nnt):
                                    P.mm(ps[banks[nt]][:, :], rg[:, wsel, k, :], XB[:, k, nt * 512:(nt + 1) * 512],
                                         k == 0, k == 7, reads=[rk, xb_keys[k // 4]], writes=[PSK[banks[nt]]])
                        for nt in range(nnt):
                            sgt = SGT[sg % 2]
                            sgk = "SGT%d" % (sg % 2)
                            sg += 1
                            gps, hps = ps[gb[nt]], ps[hb[nt]]
                            P.op("act", lambda e, sgt=sgt, gps=gps: e.activation(out=sgt, in_=gps[:, :], func=AF.Silu),
                                 reads=[PSK[gb[nt]]], writes=[sgk])
                            atv = AT[:, c, nt * 512:(nt + 1) * 512]
                            P.op("dve", lambda e, atv=atv, hps=hps, sgt=sgt: e.tensor_tensor(
                                out=atv, in0=hps[:, :], in1=sgt, op=ALU.mult),
                                 reads=[PSK[hb[nt]], sgk], writes=["AT%d_%d" % (c, nt)])
                drain(S)
                if ti + 1 < len(tiles):
                    load_xb(ti + 1)
                for f in range(8):
                    issue_w(wpos + PFW)
                    slot = wpos % 6
                    wpos += 1
                    rgo = rgo_view(slot)
                    rk = "RING%d" % slot
                    yb = [0, 1] if f % 2 == 0 else [2, 3]
                    for c in range(22):
                        for nt in range(nnt):
                            P.mm(ps[yb[nt]][:, :], rgo[:, c, :], AT[:, c, nt * 512:(nt + 1) * 512],
                                 c == 0, c == 21, reads=[rk, "AT%d_%d" % (c, nt)], writes=[PSK[yb[nt]]])
                    for nt in range(nnt):
                        epi_a(S, f, ps[yb[nt]][:, :], PSK[yb[nt]], XF[:, f, nt * 512:(nt + 1) * 512],
                              xf_keys[nt][f], 4 + nt, 6 + nt, use_pool=False, lag=2)
                for nt in range(nnt):
                    ta = t0 + nt * 512
                    if final_out:
                        d32 = fm(yT)[:, :, ta - 512:ta]
                        d16 = None
                    else:
                        d32 = fm(XS[dst])[:, :, ta:ta + 512]
                        d16 = fm(XSb[dst])[:, :, ta:ta + 512]
                    nxt = None
                    if nt == nnt - 1 and ti + 1 < len(tiles):
                        nxt = lambda ti=ti: load_xf(ti + 1)
                    epi_b(S, XF[:, :, nt * 512:(nt + 1) * 512], xf_keys[nt], 4 + nt, 6 + nt, 2, 3, layer,
                          d32, d16, final=final_out, use_pool=False, nt=nt, after=nxt, defer_head=True)
            drain(S)

        def sg_pass(li, layer, src, ta, tb, dst):
            P.barrier()
            A.reset()
            WR = A.alloc([128, 8, 3072], BF16)
            XBs = [A.alloc([128, 8, 512], BF16) for _ in range(2)]
            XF = A.alloc([128, 8, 512], F32)
            U = A.alloc([128, 8, 512], F32)
            GT = A.alloc([128, 8, 512], BF16)
            VG = [A.alloc([128, 1024], F32) for _ in range(4)]
            VNB = [A.alloc([128, 1024], BF16) for _ in range(4)]
            BS2 = A.alloc([128, 8, 128], F32)
            WS = A.alloc([128, 16, 128], BF16)
            BS = A.alloc([128, 8, 128], F32)
            MIX = [A.alloc([128, 8, 128], F32) for _ in range(2)]
            ST = A.alloc([128, 4, 12], F32)
            MV = A.alloc([128, 4, 2], F32)
            RS = A.alloc([128, 4], F32)
            NBV = A.alloc([128, 4], F32)
            S = ln_alloc(1)
            tlist = list(range(ta, tb, 512))
            ntl = len(tlist)
            xf_keys = ["XF_%d" % f for f in range(8)]

            def load_xb(ti):
                t0 = tlist[ti]
                P.dma("sp", XBs[ti % 2], fm(XSb[src])[:, :, t0:t0 + 512], writes=["XB%d" % (ti % 2)])

            def load_xf(ti):
                t0 = tlist[ti]
                for hf in range(2):
                    P.dma("sp", XF[:, 4 * hf:4 * hf + 4, :], fm(XS[src])[:, 4 * hf:4 * hf + 4, t0:t0 + 512],
                          writes=xf_keys[4 * hf:4 * hf + 4], slot="XFld%d" % hf)

            load_xb(0)
            wi3 = wsgi[li].rearrange("p (k n) -> p k n", k=8)
            wo3 = wsgo[li].rearrange("p (k n) -> p k n", k=8)
            for j in (2, 3, 0, 1):
                P.dma("pool", WR[:, :, j * 512:(j + 1) * 512], wi3[:, :, j * 512:(j + 1) * 512],
                      writes=["WR%d" % j])
            P.dma("pool", WS.rearrange("p a b -> p (a b)"), wsT[li], writes=["WS"])
            for j in (4, 5):
                P.dma("pool", WR[:, :, j * 512:(j + 1) * 512], wo3[:, :, (j - 4) * 512:(j - 3) * 512],
                      writes=["WR%d" % j])
            P.dma("sp", BS.rearrange("p a b -> p (a b)"), bsb[li], writes=["BS"])
            load_xf(0)
            if ntl > 1:
                load_xb(1)
            bank = [0]
            mcnt = [0]
            ybanks = [5, 2]
            ycnt = [0]

            def v_chunks(ti, tcs):
                XB = XBs[ti % 2]
                xk = "XB%d" % (ti % 2)
                for tc in tcs:
                    vg = VG[tc]
                    for hf in range(2):
                        b = bank[0] % 2
                        bank[0] += 1
                        for k in range(8):
                            P.mm(ps[b][:, :], XB[:, k, tc * 128:(tc + 1) * 128],
                                 WR[:, k, 1024 + hf * 512:1024 + (hf + 1) * 512], k == 0, k == 7,
                                 reads=["WR%d" % (2 + hf), xk], writes=[PSK[b]])
                        vgh = vg[:, hf * 512:(hf + 1) * 512]
                        P.op("act", lambda e, vgh=vgh, b=b: e.activation(out=vgh, in_=ps[b][:, :], func=AF.Gelu),
                             reads=[PSK[b]], writes=["VG%d_%d" % (tc, hf)])
                        sto = ST[:, tc, hf * 6:(hf + 1) * 6]
                        P.op("dve", lambda e, sto=sto, vgh=vgh: e.bn_stats(sto, vgh),
                             reads=["VG%d_%d" % (tc, hf)], writes=["ST%d_%d" % (tc, hf)])
                    P.op("dve", lambda e, tc=tc: e.bn_aggr(MV[:, tc, :], ST[:, tc, :]),
                         reads=["ST%d_0" % tc, "ST%d_1" % tc], writes=["MV%d" % tc])

            mvk = ["MV%d" % tc for tc in range(4)]

            def rstd():
                P.op("dve", lambda e: e.tensor_scalar_add(out=RS[:, :], in0=MV[:, :, 1], scalar1=EPS),
                     reads=mvk, writes=["RS"])
                P.op("act", lambda e: e.activation(out=RS[:, :], in_=RS[:, :], func=AF.Ln),
                     reads=["RS"], writes=["RS"])
                P.op("act", lambda e: e.activation(out=RS[:, :], in_=RS[:, :], func=AF.Exp, scale=-0.5),
                     reads=["RS"], writes=["RS"])
                P.op("dve", lambda e: e.scalar_tensor_tensor(out=NBV[:, :], in0=MV[:, :, 0], scalar=-1.0,
                                                             in1=RS[:, :], op0=ALU.mult, op1=ALU.mult),
                     reads=mvk + ["RS"], writes=["NBV"])

            v_chunks(0, range(4))
            rstd()
            rbanks = [0, 1, 2, 5]
            wsf = WS.rearrange("p a b -> p (a b)")
            for j in range(4):
                P.mm(ps[rbanks[j]][:, :], ONES128[:, :], wsf[:, j * 512:(j + 1) * 512], True, True,
                     reads=["ones128", "WS"], writes=[PSK[rbanks[j]]])
            for fc in range(8):
                for hh in range(2):
                    g = 2 * fc + hh
                    rb = rbanks[g // 4]
                    col = (g % 4) * 128
                    bcol = (li * 2 + 1) * 8 + fc
                    sl = slice(hh * 64, (hh + 1) * 64)
                    P.op("dve", lambda e, sl=sl, rb=rb, col=col, bcol=bcol, fc=fc: e.scalar_tensor_tensor(
                        out=BS2[sl, fc, :], in0=ps[rb][sl, col:col + 128], scalar=SGFM[sl, bcol:bcol + 1],
                        in1=BS[sl, fc, :], op0=ALU.mult, op1=ALU.add),
                         reads=[PSK[rb], "sgfm", "BS"], writes=["BS2_%d_%d" % (fc, hh)])
            bs2_keys = ["BS2_%d_%d" % (fc, hh) for fc in range(8) for hh in range(2)]
            for ti, t0 in enumerate(tlist):
                XB = XBs[ti % 2]
                xk = "XB%d" % (ti % 2)
                for fo in range(8):
                    b = bank[0] % 2
                    bank[0] += 1
                    for k in range(8):
                        P.mm(ps[b][:, :], WR[:, k, fo * 128:(fo + 1) * 128], XB[:, k, :], k == 0, k == 7,
                             reads=["WR%d" % (fo // 4), xk], writes=[PSK[b]])
                    uv = U[:, fo, :]
                    P.op("act", lambda e, uv=uv, b=b: e.activation(out=uv, in_=ps[b][:, :], func=AF.Gelu),
                         reads=[PSK[b]], writes=["U%d" % fo])
                    drain(S, 2 if fo == 0 else 1)
                drain(S)
                for tc in range(4):
                    vg, vnb = VG[tc], VNB[tc]
                    P.op("dve", lambda e, vg=vg, vnb=vnb, tc=tc: e.tensor_scalar(
                        out=vnb, in0=vg, scalar1=RS[:, tc:tc + 1], scalar2=NBV[:, tc:tc + 1],
                        op0=ALU.mult, op1=ALU.add),
                         reads=["VG%d_0" % tc, "VG%d_1" % tc, "RS", "NBV"], writes=["VNB%d" % tc])
                if ti + 1 < ntl:
                    if ti + 2 < ntl:
                        load_xb(ti + 2) if False else None
                    v_chunks(ti + 1, (0, 1))
                for tc in range(4):
                    vnb = VNB[tc]
                    ci = mcnt[0] % 2
                    mcnt[0] += 1
                    for fc in range(8):
                        mb = 3 + fc // 4
                        col = (fc % 4) * 128
                        for hh in range(2):
                            g = 2 * fc + hh
                            P.mm(ps[mb][hh * 64:(hh + 1) * 64, col:col + 128], vnb[:, g * 64:(g + 1) * 64],
                                 WS[:, g, :], True, True, reads=["VNB%d" % tc, "WS"],
                                 writes=["M%d_%d" % (mb, hh)], tp=(0, 64 * hh))
                    mix = MIX[ci]
                    for fc in range(8):
                        mb = 3 + fc // 4
                        col = (fc % 4) * 128
                        gcol = (li * 2) * 8 + fc
                        P.op("dve", lambda e, mix=mix, mb=mb, col=col, gcol=gcol, fc=fc: e.scalar_tensor_tensor(
                            out=mix[:, fc, :], in0=ps[mb][:, col:col + 128], scalar=SGFM[:, gcol:gcol + 1],
                            in1=BS2[:, fc, :], op0=ALU.mult, op1=ALU.add),
                             reads=["M%d_0" % mb, "M%d_1" % mb, "sgfm", "BS2_%d_0" % fc, "BS2_%d_1" % fc],
                             writes=["MIX%d_%d" % (ci, fc)])
                    gv = GT[:, :, tc * 128:(tc + 1) * 128]
                    uvv = U[:, :, tc * 128:(tc + 1) * 128]
                    P.op("pool", lambda e, gv=gv, mix=mix, uvv=uvv: e.tensor_tensor(out=gv, in0=mix, in1=uvv,
                                                                                 op=ALU.mult),
                         reads=["MIX%d_%d" % (ci, fc) for fc in range(8)] + ["U%d" % fo for fo in range(8)],
                         writes=["GT%d" % tc])
                if ti + 1 < ntl:
                    v_chunks(ti + 1, (2, 3))
                gt_keys = ["GT%d" % tc for tc in range(4)]
                if ti + 1 < ntl:
                    rstd()
                for f in range(8):
                    yb = ybanks[ycnt[0] % 2]
                    ycnt[0] += 1
                    for k in range(8):
                        P.mm(ps[yb][:, :], WR[:, k, 2048 + f * 128:2048 + (f + 1) * 128], GT[:, k, :], k == 0, k == 7,
                             reads=["WR%d" % (4 + f // 4)] + gt_keys, writes=[PSK[yb]])
                    epi_a(S, f, ps[yb][:, :], PSK[yb], XF[:, f, :], xf_keys[f], 6, 7, use_pool=False, lag=2)
                nxt = (lambda ti=ti: load_xf(ti + 1)) if ti + 1 < ntl else None
                epi_b(S, XF, xf_keys, 6, 7, 0, 1, layer,
                      fm(XS[dst])[:, :, t0:t0 + 512], fm(XSb[dst])[:, :, t0:t0 + 512], use_pool=False,
                      after=nxt)
                if ti + 1 < ntl:
                    if ti + 2 < ntl:
                        load_xb(ti + 2)
            drain(S)

        big = [(256 + 1024 * i, 1024) for i in range(4)] + [(4352, 512)]
        small = [(512 + 1024 * i, 1024) for i in range(4)]
        qkv_pass(0, None, 0, NTOK)
        att_pass(0, 0, x0T, [256 + 512 * b for b in range(9)], {0: "s0", 8: "e0"}, 0)
        ffn_pass(0, 0, big, 1)
        sg_pass(0, 1, 1, 256, 4864, 0)
        ffn_pass(1, 0, big, 1)
        qkv_pass(1, XSb[1], 256, 4864)
        att_pass(1, 2, XS[1], [512 + 512 * b for b in range(8)], {0: "s2", 7: "e2"}, 0)
        ffn_pass(2, 0, small, 1)
        sg_pass(1, 3, 1, 512, 4608, 0)
        ffn_pass(3, 0, small, None, final_out=True)
        P.emit()
    return nc


def _core_geom(i):
    if i < 4:
        return 0, None, 64 * i, 256
    j = i - 4
    return 1, j // 2, 64 * (j % 2), 128


def _shared_inputs(na_w_in, na_rpb, na_w_out, sg_w_in, sg_ln_g, sg_ln_b, sg_w_s, sg_b_s, sg_w_out,
                   ln_mix_g, ln_mix_b, ffn_w_in, ffn_w_out, ln_ffn_g, ln_ffn_b):
    f32 = np.float32
    c = np.ascontiguousarray

    def kmaj(w, n):
        L = w.shape[0]
        return c(w.reshape(L, 8, 128, n).transpose(0, 2, 1, 3).reshape(L, 128, 8 * n)).astype(f32)

    sh = {}
    sh["wqkv"] = kmaj(na_w_in, 3072)
    sh["wona"] = kmaj(na_w_out, 1024)
    sh["wsgi"] = kmaj(sg_w_in, 2048)
    sh["wsgo"] = kmaj(sg_w_out, 1024)
    sh["wsT"] = c(sg_w_s.transpose(0, 3, 1, 2).reshape(2, 128, 16 * 128)).astype(f32)
    bs = sg_b_s.reshape(2, 8, 2, 128)
    bs = np.repeat(bs[:, :, :, None, :], 64, axis=3)
    sh["bsb"] = c(bs.reshape(2, 8, 128, 128).transpose(0, 2, 1, 3).reshape(2, 128, 8 * 128)).astype(f32)
    gb = np.stack([sg_ln_g, sg_ln_b], axis=1)
    sh["sgfm"] = c(gb.reshape(2, 2, 8, 128).transpose(3, 0, 1, 2).reshape(128, 32)).astype(f32)
    w = ffn_w_in.reshape(4, 8, 128, 2, 11, 2, 128)
    sh["wfi"] = c(w.transpose(0, 4, 2, 5, 3, 1, 6).reshape(4, 11, 128, 4 * 8 * 128)).astype(f32)
    w = ffn_w_out.reshape(4, 22, 128, 8, 128)
    sh["wfo"] = c(w.transpose(0, 3, 2, 1, 4).reshape(4, 8, 128, 22 * 128)).astype(f32)
    tabs = np.stack([ln_mix_g, ln_mix_b, ln_ffn_g, ln_ffn_b], axis=0)
    sh["lnp"] = c(tabs.reshape(4, 4, 8, 128).transpose(3, 0, 1, 2).reshape(128, 128)).astype(f32)
    kc = np.arange(64)[:, None]
    qc = np.arange(64)[None, :]
    cs = np.clip(qc - 8, 0, 48)
    colvalid = (kc >= cs) & (kc < cs + 16)
    dcol = np.clip(kc - qc + 15, 0, 30)
    trt = np.zeros((2, 16, 128, 22, 64), f32)
    for n in range(22):
        delta = 10 - n
        for half in range(2):
            dr = delta + half
            if abs(dr) <= 7:
                vals = na_rpb[:, :, dr + 7, :][:, :, dcol]
                vals = np.where(colvalid[None, None], vals, f32(NEG))
            else:
                vals = np.broadcast_to(np.where(colvalid, f32(0.0), f32(NEG))[None, None], (2, 16, 64, 64))
            trt[:, :, half * 64:(half + 1) * 64, n, :] = vals
    sh["trt"] = c(trt.reshape(2, 16, 128, 22 * 64))
    qd = np.zeros((8, 8, 64), f32)
    for m in range(8):
        qd[m, m, :] = 1.0
    sh["qdl"] = qd.reshape(8, 512)
    return sh


def _core_inputs(i, x_prompt, x_sample):
    f32 = np.float32
    grp, b, a, rows = _core_geom(i)
    x = x_prompt[0] if grp == 0 else x_sample[b]
    xg = x.reshape(rows, 64, D)
    buf = np.zeros((80, 64, D), f32)
    lo = max(0, a - 8)
    hi = min(rows, a + 72)
    buf[lo - (a - 8):hi - (a - 8)] = xg[lo:hi]
    x0T = np.ascontiguousarray(buf.reshape(NTOK, D).T)
    msk = np.zeros((2, 9, 8, 16, 64), f32)
    for li in range(2):
        nb = 9 if li == 0 else 8
        for bi in range(nb):
            s = (-4 + 8 * bi) if li == 0 else 8 * bi
            for m in range(8):
                ql = s + m
                qr = a + ql
                for n in range(16):
                    kl = s - 4 + n
                    kr = a + kl
                    if 0 <= qr < rows:
                        rs = min(max(qr - 4, 0), rows - 8)
                        valid = rs <= kr < rs + 8
                    else:
                        valid = -4 <= kl - ql <= 3
                    if not valid:
                        msk[li, bi, m, n, :] = NEG
                    else:
                        kind = {(0, 0): "s0", (0, 8): "e0", (1, 0): "s2", (1, 7): "e2"}.get((li, bi))
                        i0, i1 = band_ranges(kind)[n // 2]
                        assert i0 <= m <= i1, ("window outside computed band", i, li, bi, m, n)
    return {"x0T": x0T, "msk": msk.reshape(2, 9, 8, 1024)}


_NC_CACHE = {}


def kernel(x_prompt, x_sample, na_w_in, na_rpb, na_w_out, sg_w_in, sg_ln_g, sg_ln_b,
           sg_w_s, sg_b_s, sg_w_out, ln_mix_g, ln_mix_b, ffn_w_in, ffn_w_out, ln_ffn_g, ln_ffn_b):
    args = [np.asarray(v, dtype=np.float32) for v in (
        x_prompt, x_sample, na_w_in, na_rpb, na_w_out, sg_w_in, sg_ln_g, sg_ln_b,
        sg_w_s, sg_b_s, sg_w_out, ln_mix_g, ln_mix_b, ffn_w_in, ffn_w_out, ln_ffn_g, ln_ffn_b)]
    x_prompt, x_sample = args[0], args[1]
    shared = _shared_inputs(*args[2:])
    in_maps = []
    for i in range(8):
        m = dict(shared)
        m.update(_core_inputs(i, x_prompt, x_sample))
        in_maps.append(m)
    if "nc" not in _NC_CACHE:
        _NC_CACHE["nc"] = build_program()
    nc = _NC_CACHE["nc"]
    res = run_bass_kernel_spmd(nc, in_maps, core_ids=list(range(8)))
    y_prompt = np.zeros((1, 16384, D), np.float32)
    y_sample = np.zeros((2, 8192, D), np.float32)
    for i in range(8):
        grp, b, a, rows = _core_geom(i)
        yt = np.asarray(res.results[i]["yT"], dtype=np.float32)
        blk = yt.T
        if grp == 0:
            y_prompt[0, a * 64:a * 64 + 4096] = blk
        else:
            y_sample[b, a * 64:a * 64 + 4096] = blk
    return (y_prompt, y_sample)
```
